# Optimizing a Trainium2 kernel written in Bass

```python
import jax, jax.numpy as jnp
from jax import lax
import numpy as np

D_MODEL = 1024
BATCH = 8
SEQ = 8192
DEPTH = 4

HEAD_DIM = 64
SB_WIDTH = D_MODEL // 2
SB_HEADS = SB_WIDTH // HEAD_DIM
Q_BLOCK = 128
SB_WINDOW = 512
KV_SPAN = SB_WINDOW + Q_BLOCK
POOL_WIDTH = D_MODEL // 4
POOL_WINDOWS = (2, 4, 8, 16)
POOL_GROUPS = len(POOL_WINDOWS)
POOL_GROUP_DIM = POOL_WIDTH // POOL_GROUPS
SG_WIDTH = D_MODEL // 4
SG_HEADS = 4
SG_HEAD_DIM = SG_WIDTH // SG_HEADS
CHUNK = 128
MIX_WIDTH = SB_WIDTH + POOL_WIDTH + SG_WIDTH
IN_WIDTH = 3 * SB_WIDTH + POOL_WIDTH + 2 * SG_WIDTH
SPLITS = tuple(int(i) for i in np.cumsum([SB_WIDTH, SB_WIDTH, SB_WIDTH, POOL_WIDTH, SG_WIDTH]))
D_FF = -(-8 * D_MODEL // (3 * 256)) * 256
N_MOD = 6
EPS = 1e-6

kernel_name = 'hybrid_sb_pool_sgmlp_adaln_trunk'


def rmsnorm(x, g):
    xf = x.astype(jnp.float32)
    y = xf * lax.rsqrt(jnp.mean(xf * xf, axis=-1, keepdims=True) + EPS)
    return (y * g.astype(jnp.float32)).astype(x.dtype)


def layernorm(x, g):
    xf = x.astype(jnp.float32)
    mu = jnp.mean(xf, axis=-1, keepdims=True)
    var = jnp.mean(jnp.square(xf - mu), axis=-1, keepdims=True)
    return ((xf - mu) * lax.rsqrt(var + EPS) * g.astype(jnp.float32)).astype(x.dtype)


def stick_breaking_attention(q, k, v):
    b, s, h, d = q.shape
    nb = s // Q_BLOCK
    qb = q.reshape(b, nb, Q_BLOCK, h, d).transpose(1, 0, 2, 3, 4)
    pad = ((0, 0), (SB_WINDOW, 0), (0, 0), (0, 0))
    kp = jnp.pad(k.astype(jnp.float32), pad)
    vp = jnp.pad(v.astype(jnp.float32), pad)
    scale = d ** -0.5

    def one_block(args):
        q_blk, i = args
        start = i * Q_BLOCK
        k_blk = lax.dynamic_slice_in_dim(kp, start, KV_SPAN, axis=1)
        v_blk = lax.dynamic_slice_in_dim(vp, start, KV_SPAN, axis=1)
        q_pos = i * Q_BLOCK + jnp.arange(Q_BLOCK)
        key_pos = i * Q_BLOCK - SB_WINDOW + jnp.arange(KV_SPAN)
        valid = ((key_pos[None, :] < q_pos[:, None])
                 & (key_pos[None, :] >= q_pos[:, None] - SB_WINDOW)
                 & (key_pos[None, :] >= 0))
        z = jnp.einsum('bqhd,bkhd->bhqk', q_blk.astype(jnp.float32), k_blk) * scale
        log_rest = jnp.where(valid, jax.nn.log_sigmoid(-z), 0.0)
        later = lax.cumsum(log_rest, axis=3, reverse=True) - log_rest
        w = jnp.where(valid, jnp.exp(log_rest + z + later), 0.0)
        return jnp.einsum('bhqk,bkhd->bqhd', w, v_blk)

    out = lax.map(one_block, (qb, jnp.arange(nb)))
    return out.transpose(1, 0, 2, 3, 4).reshape(b, s, h * d).astype(q.dtype)


def multiscale_pool(p, w_pool, pool_scale):
    b, s, _ = p.shape
    pg = p.astype(jnp.float32).reshape(b, s, POOL_GROUPS, POOL_GROUP_DIM)
    maxw = max(POOL_WINDOWS)
    csp = jnp.pad(jnp.cumsum(pg, axis=1), ((0, 0), (maxw, 0), (0, 0), (0, 0)))
    pos = jnp.arange(s)
    outs = []
    for g, w in enumerate(POOL_WINDOWS):
        window_sum = csp[:, maxw:, g] - csp[:, maxw - w:maxw - w + s, g]
        count = jnp.minimum(pos + 1, w).astype(jnp.float32)[None, :, None]
        outs.append(window_sum / count - pg[:, :, g])
    pooled = jnp.stack(outs, axis=2)
    mixed = jnp.einsum('bsgc,gce->bsge', pooled, w_pool.astype(jnp.float32))
    return (mixed.reshape(b, s, POOL_WIDTH) * pool_scale).astype(p.dtype)


def chunked_spatial_gating(u, vg, sg_norm, w_s, b_s):
    b, s, _ = u.shape
    n = s // CHUNK
    u = jax.nn.gelu(u)
    vn = layernorm(jax.nn.gelu(vg), sg_norm)
    vh = vn.reshape(b, n, CHUNK, SG_HEADS, SG_HEAD_DIM)
    causal = jnp.tril(jnp.ones((CHUNK, CHUNK), dtype=bool))
    ws = jnp.where(causal[None], w_s, 0.0)
    mixed = jnp.einsum('hts,bnshd->bnthd', ws, vh) + b_s.T[None, None, :, :, None]
    return u * mixed.reshape(b, s, SG_WIDTH)


def setup_inputs(seed: int = 0) -> dict:
    key = jax.random.key(seed)
    ks = jax.random.split(key, 20)
    f32 = jnp.float32
    nrm = lambda k, shape, s: jax.random.normal(k, shape, f32) * s
    return {
        'x': nrm(ks[0], (BATCH, SEQ, D_MODEL), 1.0),
        'c': nrm(ks[1], (BATCH, D_MODEL), 1.0),
        'w_ada': nrm(ks[2], (DEPTH, D_MODEL, N_MOD * D_MODEL), 0.5 * D_MODEL ** -0.5),
        'b_ada': nrm(ks[3], (DEPTH, N_MOD * D_MODEL), 0.02),
        'norm_mix_in': 1.0 + nrm(ks[4], (DEPTH, D_MODEL), 0.02),
        'w_in': nrm(ks[5], (DEPTH, D_MODEL, IN_WIDTH), D_MODEL ** -0.5),
        'w_pool': nrm(ks[6], (DEPTH, POOL_GROUPS, POOL_GROUP_DIM, POOL_GROUP_DIM), POOL_GROUP_DIM ** -0.5),
        'pool_scale': 1.0 + nrm(ks[7], (DEPTH, POOL_WIDTH), 0.1),
        'sg_norm': 1.0 + nrm(ks[8], (DEPTH, SG_WIDTH), 0.02),
        'w_s': nrm(ks[9], (DEPTH, SG_HEADS, CHUNK, CHUNK), CHUNK ** -0.5),
        'b_s': 1.0 + nrm(ks[10], (DEPTH, SG_HEADS, CHUNK), 0.02),
        'mix_norm': 1.0 + nrm(ks[11], (DEPTH, MIX_WIDTH), 0.02),
        'w_out': nrm(ks[12], (DEPTH, MIX_WIDTH, D_MODEL), MIX_WIDTH ** -0.5),
        'norm_ffn_in': 1.0 + nrm(ks[13], (DEPTH, D_MODEL), 0.02),
        'w_gate_up': nrm(ks[14], (DEPTH, D_MODEL, 2 * D_FF), D_MODEL ** -0.5),
        'w_down': nrm(ks[15], (DEPTH, D_FF, D_MODEL), D_FF ** -0.5),
        'final_norm': 1.0 + nrm(ks[16], (D_MODEL,), 0.02),
    }


def reference(x, c, w_ada, b_ada, norm_mix_in, w_in, w_pool, pool_scale, sg_norm, w_s, b_s,
              mix_norm, w_out, norm_ffn_in, w_gate_up, w_down, final_norm):
    b, s, _ = x.shape
    c_act = jax.nn.silu(c)
    for l in range(DEPTH):
        mod = (c_act @ w_ada[l] + b_ada[l])[:, None, :]
        sh1, sc1, g1, sh2, sc2, g2 = jnp.split(mod, N_MOD, axis=-1)

        h = rmsnorm(x, norm_mix_in[l]) * (1.0 + sc1) + sh1
        proj = h @ w_in[l]
        q, k, v, p, u, vg = jnp.split(proj, SPLITS, axis=-1)
        a_out = stick_breaking_attention(q.reshape(b, s, SB_HEADS, HEAD_DIM),
                                         k.reshape(b, s, SB_HEADS, HEAD_DIM),
                                         v.reshape(b, s, SB_HEADS, HEAD_DIM))
        p_out = multiscale_pool(p, w_pool[l], pool_scale[l])
        g_out = chunked_spatial_gating(u, vg, sg_norm[l], w_s[l], b_s[l])
        mn = mix_norm[l]
        merged = jnp.concatenate([
            rmsnorm(a_out, mn[:SB_WIDTH]),
            rmsnorm(p_out, mn[SB_WIDTH:SB_WIDTH + POOL_WIDTH]),
            rmsnorm(g_out, mn[SB_WIDTH + POOL_WIDTH:]),
        ], axis=-1)
        x = x + g1 * (merged @ w_out[l])

        h2 = rmsnorm(x, norm_ffn_in[l]) * (1.0 + sc2) + sh2
        gate, up = jnp.split(h2 @ w_gate_up[l], 2, axis=-1)
        x = x + g2 * ((jax.nn.silu(gate) * up) @ w_down[l])
    return rmsnorm(x, final_norm)
```

```python
import numpy as np
from contextlib import ExitStack
import concourse.bass as bass
import concourse.mybir as mybir
from concourse.bass_utils import run_bass_kernel_spmd

F32 = mybir.dt.float32
BF16 = mybir.dt.bfloat16
AF = mybir.ActivationFunctionType
ALU = mybir.AluOpType

D = 1024
SEQ = 8192
DEPTH = 4
TT = 512
NKC = 8
NF = 22
DFF = 2816
NU = 51
NSLOT = 8
EPS = 1e-6
SEM_LIMIT = 6000
SAME_ENGINE_SYNC = True


class Eng:
    def __init__(self, sched, name):
        self.s = sched
        self.name = name
        self.ops = []
        self.sem = None
        self.cnt = 0
        self.nsem = 0
        self.waited = {}

    def _cur_sem(self):
        if self.sem is None or self.cnt >= SEM_LIMIT:
            self.sem = self.s.new_sem(f"{self.name}{self.nsem}")
            self.nsem += 1
            self.cnt = 0
        return self.sem

    def _waits(self, deps):
        best = {}

        def add(d):
            if d is None:
                return
            if isinstance(d, list):
                for dd in d:
                    add(dd)
                return
            k = d[0].name
            if k not in best or d[1] > best[k][1]:
                best[k] = d

        for d in deps:
            add(d)
        waits = []
        for k, (sem, val) in best.items():
            if self.waited.get(k, 0) >= val:
                continue
            self.waited[k] = val
            waits.append((sem, val))
        return waits

    def emit(self, fn, deps=(), inc=True):
        waits = self._waits(deps)
        tok = None
        incinfo = None
        if inc:
            sem = self._cur_sem()
            self.cnt += 1
            tok = (sem, self.cnt)
            incinfo = (sem, 1)
            if not SAME_ENGINE_SYNC:
                self.waited[sem.name] = self.cnt
        self.ops.append((waits, fn, incinfo))
        return tok

    def dma(self, fn, dsem, deps=()):
        waits = self._waits(deps)
        dsem.cnt += 16
        tok = (dsem.sem, dsem.cnt)
        self.ops.append((waits, fn, (dsem.sem, 16)))
        return tok

    def final_wait(self, toks):
        self.ops.append((self._waits(toks), None, None))

    def replay(self, e):
        for waits, fn, incinfo in self.ops:
            for sem, val in waits:
                e.wait_ge(sem, val)
            if fn is None:
                continue
            ins = fn(e)
            if incinfo is not None:
                ins.then_inc(incinfo[0], incinfo[1])
        self.ops = []


class DmaSem:
    def __init__(self, sched, name):
        self.sem = sched.new_sem(name)
        self.cnt = 0


class Sched:
    def __init__(self, nc, es):
        self.nc = nc
        self.es = es
        self.nsems = 0
        self.pe = Eng(self, "pe")
        self.act = Eng(self, "act")
        self.dve = Eng(self, "dve")
        self.pool = Eng(self, "pool")
        self.sp = Eng(self, "sp")

    def new_sem(self, name):
        self.nsems += 1
        return self.es.enter_context(self.nc.semaphore(f"s_{name}"))

    def dsem(self, name):
        return DmaSem(self, name)

    def sb(self, name, shape, dtype):
        return self.es.enter_context(self.nc.sbuf_tensor("sb_" + name, list(shape), dtype))

    def ps(self, name, shape, dtype=F32):
        return self.es.enter_context(self.nc.psum_tensor("pm_" + name, list(shape), dtype))

    def run(self):
        block = self.es.enter_context(self.nc.Block())

        @block.tensor
        def _(e):
            self.pe.replay(e)

        @block.scalar
        def _(e):
            self.act.replay(e)

        @block.vector
        def _(e):
            self.dve.replay(e)

        @block.gpsimd
        def _(e):
            self.pool.replay(e)

        @block.sync
        def _(e):
            self.sp.replay(e)


class Tk:
    def __init__(self):
        self.w = None
        self.r = []

    def rd(self):
        return [self.w]

    def wr(self):
        return [self.w] + self.r

    def wrote(self, tok):
        self.w = tok
        self.r = []

    def read(self, tok):
        if tok is not None:
            self.r.append(tok)
            if len(self.r) > 24:
                self.r = self.r[-24:]


def build_program(ntiles=SEQ // TT, depth=DEPTH, stage=9):
    nc = bass.Bass("TRN2", target_bir_lowering=False)
    ntok = ntiles * TT
    x_d = nc.dram_tensor("x", [ntok, D], F32, kind="ExternalInput").ap()
    out_d = nc.dram_tensor("out", [ntok, D], F32, kind="ExternalOutput").ap()
    pp_d = nc.dram_tensor("pp", [128, 312], F32, kind="ExternalInput").ap()
    bc_d = nc.dram_tensor("bc", [128, 2048], F32, kind="ExternalInput").ap()
    cst_d = nc.dram_tensor("cst", [128, 128 + 640 + 128 + 32], F32, kind="ExternalInput").ap()
    wsT_d = nc.dram_tensor("wsT", [128, 2048], F32, kind="ExternalInput").ap()
    wp_d = nc.dram_tensor("wp", [128, 512], F32, kind="ExternalInput").ap()
    wada_d = nc.dram_tensor("w_ada", [DEPTH, D, 6 * D], F32, kind="ExternalInput").ap()
    win_d = nc.dram_tensor("w_in", [DEPTH, D, 2304], F32, kind="ExternalInput").ap()
    wout_d = nc.dram_tensor("w_out", [DEPTH, D, D], F32, kind="ExternalInput").ap()
    wgu_d = nc.dram_tensor("w_gate_up", [DEPTH, D, 2 * DFF], F32, kind="ExternalInput").ap()
    wdn_d = nc.dram_tensor("w_down", [DEPTH, DFF, D], F32, kind="ExternalInput").ap()
    wscr = nc.dram_tensor("wscr", [DEPTH * NU, 128, 2048], BF16).ap()
    kscr = nc.dram_tensor("kscr", [DEPTH, 128, 4, 512], BF16).ap()
    vscr = nc.dram_tensor("vscr", [DEPTH, 128, 4, 512], BF16).ap()
    pscr = nc.dram_tensor("pscr", [DEPTH, 128, 2, 16], F32).ap()

    with ExitStack() as es:
        S = Sched(nc, es)
        PE, ACT, DVE, POOL, SP = S.pe, S.act, S.dve, S.pool, S.sp

        def op(eng, name, deps, *a, inc=True, **kw):
            return eng.emit(lambda e: getattr(e, name)(*a, **kw), deps, inc=inc)

        def MM(out, lhsT, rhs, start=True, stop=True, deps=(), inc=False):
            return PE.emit(lambda e: e.matmul(out, lhsT=lhsT, rhs=rhs, start=start, stop=stop), deps, inc=inc)

        def TR(out, in_, ident, deps=(), inc=False):
            return PE.emit(lambda e: e.transpose(out, in_, ident), deps, inc=inc)

        def A(out, in_, func, deps=(), **kw):
            return ACT.emit(lambda e: e.activation(out=out, in_=in_, func=func, **kw), deps)

        xT = S.sb("xT", [128, 8, TT], F32)
        hT = S.sb("hT", [128, 8, TT], BF16)
        mT = S.sb("mT", [128, 8, TT], BF16)
        U = S.sb("U", [128, 5632], F32)
        qT = S.sb("qT", [128, 4, TT], BF16)
        KTw = S.sb("KTw", [128, 4, 1024], BF16)
        Vw = S.sb("Vw", [128, 8, 512], BF16)
        pT = S.sb("pT", [128, 2, 528], F32)
        L1 = S.sb("L1", [128, 2, 528], F32)
        L2 = S.sb("L2", [128, 2, 528], F32)
        pooledT = S.sb("pooledT", [128, 2, TT], BF16)
        guT = S.sb("guT", [128, 2, TT], F32)
        vn = S.sb("vn", [128, 4, 256], BF16)
        gv = S.sb("gv", [128, 4, 256], F32)
        gvc = S.sb("gvc", [128, 2, 256], F32)
        lnst = S.sb("lnst", [128, 4, 8], F32)
        att = S.sb("att", [128, 6736], F32)
        sqb = S.sb("sqb", [128, 3, TT], BF16)
        rstd = S.sb("rstd", [128, 2, TT], F32)
        silu = S.sb("silu", [128, 2, TT], F32)
        wring = S.sb("wring", [128, NSLOT, 2048], BF16)
        negS = S.sb("negS", [128, 2], F32)
        ident32 = S.sb("ident32", [128, 128], F32)
        identb = S.sb("identb", [128, 128], BF16)
        maskA = S.sb("maskA", [128, 640], F32)
        maskS = S.sb("maskS", [128, 128], F32)
        pcorr = S.sb("pcorr", [128, 2, 16], F32)
        onesb = S.sb("onesb", [128, 128], BF16)
        ones640 = S.sb("ones640", [128, 640], BF16)
        pp = S.sb("pp", [128, 312], F32)
        bc = S.sb("bc", [128, 2048], F32)
        wsTb = S.sb("wsTb", [128, 16, 128], BF16)
        wpb = S.sb("wpb", [128, 8, 64], BF16)
        cact = S.sb("cact", [128, 8], F32)
        modT = S.sb("modT", [128, DEPTH, 48], F32)
        A1 = S.sb("A1", [128, DEPTH, 8], F32)
        A2 = S.sb("A2", [128, DEPTH, 8], F32)

        xin = U[:, 0:4096].rearrange("p (t n) -> p t n", t=4)
        brT = U[:, 0:4096].rearrange("p (c n) -> p c n", c=8)
        actT = U[:].bitcast(BF16).rearrange("p (f n) -> p f n", f=NF)
        xo = U[:, 0:2048].rearrange("p (c n) -> p c n", c=4)
        xout = U[:, 2048:4096].rearrange("p (t n) -> p t n", t=4)
        attb = att[:].bitcast(BF16)
        e_buf = [att[:, k * 640:(k + 1) * 640] for k in range(3)]
        sp_buf = [att[:, 1920 + k * 640:1920 + (k + 1) * 640] for k in range(3)]
        C_b = [att[:, 3840 + k * 648:3840 + (k + 1) * 648] for k in range(2)]
        w_b = [attb[:, 10272 + k * 640:10272 + (k + 1) * 640] for k in range(2)]
        wT_b = [attb[:, 11552 + k * 640:11552 + (k + 1) * 640] for k in range(3)]
        st32 = [U[:, 0:2048], U[:, 2048:4096], att[:, 0:2048]]
        Ub = U[:].bitcast(BF16)
        stb = [Ub[:, 8192:10240], attb[:, 4096:6144], attb[:, 6144:8192]]
        negmb = S.sb("negmb", [128, 256], BF16)
        PP_C, PP_BADA, PP_NMI, PP_NFI, PP_FN, PP_MN, PP_PS = 0, 8, 200, 232, 264, 272, 304

        psA = S.ps("psA", [128, 1024])
        psB = S.ps("psB", [128, 1024])
        psT = S.ps("psT", [128, 1024], BF16)
        ps5 = S.ps("ps5", [128, 512])
        ps6 = S.ps("ps6", [128, 512])
        ps7 = S.ps("ps7", [128, 512])
        big = [(psA, Tk()), (psB, Tk())]
        small = [(ps6, Tk()), (ps7, Tk())]
        aT_bank = (ps5, Tk())
        psT_tk = Tk()
        cnt = {"big": 0, "small": 0, "ev": 0}

        def nbig():
            cnt["big"] += 1
            return big[cnt["big"] % 2]

        def nsmall():
            cnt["small"] += 1
            return small[cnt["small"] % 2]

        def evac(out, in_, deps, eng=None):
            if eng is None:
                cnt["ev"] += 1
                eng = "act" if cnt["ev"] % 2 else "dve"
            if eng == "act":
                return A(out, in_, AF.Identity, deps=deps)
            return op(DVE, "tensor_copy", deps, out=out, in_=in_)

        dl = S.dsem("cload")
        t_pp = SP.dma(lambda e: e.dma_start(out=pp[:], in_=pp_d), dl)
        t_bc = SP.dma(lambda e: e.dma_start(out=bc[:], in_=bc_d), dl)
        t_i32 = SP.dma(lambda e: e.dma_start(out=ident32[:], in_=cst_d[:, 0:128]), dl)
        t_mA = SP.dma(lambda e: e.dma_start(out=maskA[:], in_=cst_d[:, 128:768]), dl)
        t_mS = SP.dma(lambda e: e.dma_start(out=maskS[:], in_=cst_d[:, 768:896]), dl)
        t_pc = SP.dma(lambda e: e.dma_start(out=pcorr[:], in_=cst_d[:, 896:928].rearrange("p (c n) -> p c n", c=2)), dl)
        t_ws = SP.dma(lambda e: e.dma_start(out=st32[0], in_=wsT_d), dl)
        t_wp = SP.dma(lambda e: e.dma_start(out=st32[1][:, 0:512], in_=wp_d), dl)
        t_const = (dl.sem, dl.cnt)
        cd = [t_const]
        t_idb = op(DVE, "tensor_copy", cd, out=identb[:], in_=ident32[:])
        t_ones = op(DVE, "memset", [], onesb[:], 1.0)
        t_o640 = op(DVE, "memset", [], ones640[:], 1.0)
        t_nm0 = op(DVE, "tensor_scalar", cd, out=negmb[:, 0:128], in0=maskA[:, 0:128], scalar1=-1.0, scalar2=30000.0, op0=ALU.add, op1=ALU.mult)
        t_nm = op(DVE, "tensor_scalar", cd, out=negmb[:, 128:256], in0=maskA[:, 512:640], scalar1=-1.0, scalar2=30000.0, op0=ALU.add, op1=ALU.mult)
        t_wsb = None
        for i in range(16):
            t_wsb = op(DVE, "tensor_tensor", cd, out=wsTb[:, i, :], in0=st32[0][:, i * 128:(i + 1) * 128], in1=maskS[:], op=ALU.mult)
        t_wpb = op(DVE, "tensor_copy", cd, out=wpb[:].rearrange("p a b -> p (a b)"), in_=st32[1][:, 0:512])
        t_cact = A(cact[:], pp[:, PP_C:PP_C + 8], AF.Silu, deps=cd)
        st_free = [[t_wsb], [t_wpb], []]
        stb_free = [[], [], []]

        def finish(toks):
            for en in (PE, ACT, DVE, POOL, SP):
                en.final_wait(toks)
            S.run()
            return nc

        if stage == 0:
            return finish([t_const, t_idb, t_ones, t_o640, t_wsb, t_wpb, t_cact])
        ld_sem = [S.dsem(f"pl{i}") for i in range(3)]
        st_sem = [S.dsem(f"ps{i}") for i in range(3)]
        pcount = 0
        mod_ps = [ps5, ps6, ps7, psA]
        pst8 = {"pcount": 0, "t_mod_last": None, "tl": None}

        def mod_block(l, nb):
            mps = mod_ps[l % 4]
            s = pst8["pcount"] % 3
            pst8["pcount"] += 1
            src = wada_d[l].rearrange("(kc p) n -> p kc n", p=128)[:, :, nb * 256:(nb + 1) * 256]
            dst = st32[s].rearrange("p (kc n) -> p kc n", kc=8)
            t_ld = SP.dma(lambda e, dst=dst, src=src: e.dma_start(out=dst, in_=src), ld_sem[s], deps=st_free[s])
            tl = None
            for jj in range(2):
                j = nb * 2 + jj
                for kc in range(8):
                    tl = MM(mps[:, j:j + 1], lhsT=dst[:, kc, jj * 128:(jj + 1) * 128], rhs=cact[:, kc:kc + 1],
                            start=(kc == 0), stop=(kc == 7), deps=[t_ld, t_cact], inc=(jj == 1 and kc == 7))
            st_free[s] = [tl]
            if nb == 23:
                t_mod = op(DVE, "tensor_tensor", [tl, t_pp], out=modT[:, l, :], in0=mps[:, 0:48],
                           in1=pp[:, PP_BADA + l * 48:PP_BADA + (l + 1) * 48], op=ALU.add)
                t_a1 = op(DVE, "scalar_tensor_tensor", [t_mod], out=A1[:, l, :], in0=modT[:, l, 8:16], scalar=1.0,
                          in1=pp[:, PP_NMI + l * 8:PP_NMI + (l + 1) * 8], op0=ALU.add, op1=ALU.mult)
                pst8["t_mod_last"] = op(DVE, "scalar_tensor_tensor", [t_mod, t_a1], out=A2[:, l, :], in0=modT[:, l, 32:40], scalar=1.0,
                                        in1=pp[:, PP_NFI + l * 8:PP_NFI + (l + 1) * 8], op0=ALU.add, op1=ALU.mult)

        def unit_src(l, u):
            if u < 9:
                return win_d[l].rearrange("(kc p) n -> p kc n", p=128)[:, :, u * 256:(u + 1) * 256], "p (kc n) -> p kc n", dict(kc=8), 2048
            if u < 13:
                j = u - 9
                return wout_d[l].rearrange("(kc p) n -> p kc n", p=128)[:, :, j * 256:(j + 1) * 256], "p (kc n) -> p kc n", dict(kc=8), 2048
            if u < 35:
                f = u - 13
                return (wgu_d[l].rearrange("(kc p) (g f n) -> p kc g f n", p=128, g=2, n=128)[:, :, :, f, :],
                        "p (kc g n) -> p kc g n", dict(kc=8, g=2), 2048)
            k = u - 35
            o, half = k // 2, k % 2
            return (wdn_d[l].rearrange("(f p) n -> p f n", p=128)[:, half * 11:(half + 1) * 11, o * 128:(o + 1) * 128],
                    "p (f n) -> p f n", dict(f=11), 1408)

        pst9 = {"cast_rr": 0}

        def conv_unit(l, u):
            s = pst8["pcount"] % 3
            pst8["pcount"] += 1
            src, pat, kw, n = unit_src(l, u)
            dst = st32[s][:, 0:n].rearrange(pat, **kw)
            if 13 <= u < 35:
                for g_ in range(2):
                    t_ld = SP.dma(lambda e, dst=dst[:, :, g_, :], src=src[:, :, g_, :]: e.dma_start(out=dst, in_=src), ld_sem[s], deps=st_free[s])
            else:
                t_ld = SP.dma(lambda e, dst=dst, src=src: e.dma_start(out=dst, in_=src), ld_sem[s], deps=st_free[s])
            pst9["cast_rr"] += 1
            ce = [ACT, DVE, POOL][pst9["cast_rr"] % 3]
            if ce is ACT:
                t_c = A(stb[s][:, 0:n], st32[s][:, 0:n], AF.Identity, deps=[t_ld] + stb_free[s])
            else:
                t_c = op(ce, "tensor_copy", [t_ld] + stb_free[s], out=stb[s][:, 0:n], in_=st32[s][:, 0:n])
            st_free[s] = [t_c]
            t_st = POOL.dma(lambda e, s=s, n=n, idx=l * NU + u: e.dma_start(out=wscr[idx, :, 0:n], in_=stb[s][:, 0:n]),
                            st_sem[s], deps=[t_c])
            stb_free[s] = [t_st]

        mod_items = [(l, nb) for l in range(depth) for nb in range(24)]
        conv_items = [(l, u) for l in range(depth) for u in range(NU)]
        mi = ci = 0
        while mi < len(mod_items) or ci < len(conv_items):
            for _ in range(2):
                if ci < len(conv_items):
                    conv_unit(*conv_items[ci]); ci += 1
            if mi < len(mod_items):
                mod_block(*mod_items[mi]); mi += 1
        t_mod_last = pst8["t_mod_last"]
        prologue_done = [t_mod_last, t_idb, t_ones, t_o640, t_wsb, t_wpb, t_nm0, t_nm] + \
            [(st_sem[i].sem, st_sem[i].cnt) for i in range(3) if st_sem[i].cnt > 0] + \
            [st_free[i] for i in range(3)]

        t_c0 = op(DVE, "memset", prologue_done, C_b[0][:, 0:1], 0.0)
        t_c0 = op(DVE, "memset", [t_c0], C_b[1][:, 0:1], 0.0)
        prologue_done = prologue_done + [t_c0]
        if stage == 2:
            return finish(prologue_done)
        order = [(l, u) for _t in range(ntiles) for l in range(depth) for u in range(NU)]
        wsem = [S.dsem(f"w{i}") for i in range(NSLOT)]
        wst = {"issued": 0, "ld": {}, "done": {}}

        def w_issue_upto(k):
            k = min(k, len(order) - 1)
            while wst["issued"] <= k:
                i = wst["issued"]
                l, u = order[i]
                n = 1408 if u >= 35 else 2048
                s = i % NSLOT
                deps = list(prologue_done) if i < NSLOT else [wst["done"][i - NSLOT]]
                wst["ld"][i] = SP.dma(lambda e, s=s, n=n, idx=l * NU + u: e.dma_start(out=wring[:, s, 0:n], in_=wscr[idx, :, 0:n]),
                                      wsem[s], deps=deps)
                wst["issued"] += 1

        wpos = {"i": 0}

        def w_get():
            i = wpos["i"]
            wpos["i"] += 1
            w_issue_upto(i + NSLOT - 1)
            return wring[:, i % NSLOT, :], wst["ld"][i], i

        def w_done(i, tok):
            wst["done"][i] = tok

        tk_xT = [Tk() for _ in range(8)]
        tk_hT = Tk()
        hT_w = [None] * 8
        tk_mT = [Tk() for _ in range(8)]
        tk_U = Tk()
        tk_br = [Tk() for _ in range(8)]
        tk_qT = [Tk() for _ in range(4)]
        tk_Kc = [Tk() for _ in range(4)]
        tk_Kh = Tk()
        tk_Vc = [Tk() for _ in range(4)]
        tk_Vh = Tk()
        tk_pT = Tk()
        tk_ph = Tk()
        tk_gu = [Tk(), Tk()]
        tk_vn = [Tk() for _ in range(4)]
        tk_sq = [Tk() for _ in range(3)]
        tk_rstd = [Tk(), Tk()]
        tk_silu = [Tk(), Tk()]
        tk_act = [Tk() for _ in range(NF)]
        tk_att = dict(e=[Tk() for _ in range(3)], sp=[Tk() for _ in range(3)], C=[Tk(), Tk()], w=[Tk(), Tk()], wT=[Tk(), Tk(), Tk()])
        tk_pool = dict(L1=Tk(), L2=Tk(), pooled=Tk())
        tk_gv = [Tk() for _ in range(4)]
        tk_gvc = [Tk(), Tk()]
        kv_sem = S.dsem("kvs")
        kv_ld = S.dsem("kvl")
        x_ld = S.dsem("xld")
        o_st = S.dsem("ost")
        save_tok = {}
        sq_i = {"i": 0}
        rs_i = {"i": 0}

        def rms_stat(srcs, nfeat, src_deps):
            bank, btk = nsmall()
            n = len(srcs)
            tl = None
            for i, (ap, dep) in enumerate(zip(srcs, src_deps)):
                sq_i["i"] += 1
                k = sq_i["i"] % 3
                t_sq = A(sqb[:, k, :], ap, AF.Square, deps=[dep] + tk_sq[k].wr())
                tk_sq[k].wrote(t_sq)
                tl = MM(bank[:, :], lhsT=onesb[:], rhs=sqb[:, k, :], start=(i == 0), stop=(i == n - 1),
                        deps=[t_sq] + (btk.wr() if i == 0 else []), inc=True)
                tk_sq[k].read(tl)
            btk.wrote(tl)
            rs_i["i"] += 1
            r = rs_i["i"] % 2
            t1 = A(rstd[:, r, :], bank[:, :], AF.Ln, deps=[tl] + tk_rstd[r].wr(), scale=1.0 / nfeat, bias=EPS)
            btk.read(t1)
            t2 = A(rstd[:, r, :], rstd[:, r, :], AF.Exp, deps=[t1], scale=-0.5)
            tk_rstd[r].wrote(t2)
            return r, t2

        def norm_mod(l, Acol, Bcol, lastU=None):
            r, t_r = rms_stat([xT[:, c, :] for c in range(8)], D, [tk_xT[c].w for c in range(8)])
            tl = None
            for c in range(8):
                k = c % 2
                t1 = op(DVE, "tensor_tensor", [t_r, tk_xT[c].w] + tk_silu[k].wr(), out=silu[:, k, :], in0=xT[:, c, :],
                        in1=rstd[:, r, :], op=ALU.mult)
                tk_rstd[r].read(t1)
                tk_xT[c].read(t1)
                t2 = A(hT[:, c, :], silu[:, k, :], AF.Identity, deps=[t1] + (tk_hT.wr() if c == 0 else []),
                       scale=Acol(c), bias=Bcol(c))
                tk_silu[k].wrote(t1)
                tk_silu[k].read(t2)
                hT_w[c] = t2
                tl = t2
            tk_hT.wrote(tl)
            return tl

        class _Stop(Exception):
            pass

        def chk(k, toks):
            if stage == k:
                raise _Stop(toks)

        out_toks = []
        try:
            for tile in range(ntiles):
                tok0 = tile * TT
                t_x = POOL.dma(lambda e, tok0=tok0: e.dma_start(out=xin, in_=x_d[tok0:tok0 + TT, :].rearrange("(t p) n -> p t n", p=128)),
                               x_ld, deps=tk_U.wr() + (prologue_done if tile == 0 else []))
                tk_U.wrote(t_x)
                tl_x = None
                for c in range(8):
                    bank, btk = nsmall()
                    tl = None
                    for tb in range(4):
                        tl = TR(bank[:, tb * 128:(tb + 1) * 128], xin[:, tb, c * 128:(c + 1) * 128], ident32[:],
                                deps=[t_x] + (btk.wr() if tb == 0 else []), inc=(tb == 3))
                    btk.wrote(tl)
                    t_e = evac(xT[:, c, :], bank[:, :], [tl] + tk_xT[c].wr())
                    btk.read(t_e)
                    tk_xT[c].wrote(t_e)
                    tl_x = tl
                tk_U.read(tl_x)
                chk(3, [tk_xT[c].w for c in range(8)])

                for l in range(depth):
                    first_tile = (tile == 0)
                    if not first_tile:
                        t_kh = POOL.dma(lambda e, l=l: e.dma_start(out=KTw[:, :, 0:512], in_=kscr[l]), kv_ld, deps=tk_Kh.wr() + save_tok[l])
                        t_vh = POOL.dma(lambda e, l=l: e.dma_start(out=Vw[:, 0:4, :], in_=vscr[l]), kv_ld, deps=tk_Vh.wr() + save_tok[l])
                        t_ph = POOL.dma(lambda e, l=l: e.dma_start(out=pT[:, :, 0:16], in_=pscr[l]), kv_ld, deps=tk_ph.wr() + save_tok[l])
                        t_halo = (kv_ld.sem, kv_ld.cnt)
                        tk_Kh.wrote(t_halo); tk_Vh.wrote(t_halo); tk_ph.wrote(t_halo)
                    else:
                        t_ph = op(POOL, "memset", tk_ph.wr(), pT[:, :, 0:16], 0.0)
                        tk_ph.wrote(t_ph)

                    norm_mod(l, lambda c: A1[:, l, c:c + 1], lambda c: modT[:, l, c:c + 1])

                    chk(4, [tk_hT.w])
                    gv_tok = {}
                    v_ev = {}
                    for u in range(9):
                        v_rd = []
                        ws, t_w, wi = w_get()
                        wv = ws.rearrange("p (kc n) -> p kc n", kc=8)
                        tl_u = None
                        if u in (0, 1, 2, 3, 6, 7):
                            for sub in range(2):
                                bank, btk = nsmall()
                                tl = None
                                for kc in range(8):
                                    tl = MM(bank[:, :], lhsT=wv[:, kc, sub * 128:(sub + 1) * 128], rhs=hT[:, kc, :],
                                            start=(kc == 0), stop=(kc == 7),
                                            deps=[t_w, hT_w[kc]] + (btk.wr() if kc == 0 else []), inc=(kc == 7))
                                btk.wrote(tl)
                                tl_u = tl
                                if u in (0, 1):
                                    c = u * 2 + sub
                                    t_e = A(qT[:, c, :], bank[:, :], AF.Identity, deps=[tl] + tk_qT[c].wr(), scale=0.125)
                                    tk_qT[c].wrote(t_e)
                                elif u in (2, 3):
                                    c = (u - 2) * 2 + sub
                                    t_e = evac(KTw[:, c, 512:1024], bank[:, :], [tl] + tk_Kc[c].wr())
                                    tk_Kc[c].wrote(t_e)
                                elif u == 6:
                                    t_e = op(DVE, "tensor_copy", [tl] + tk_pT.wr(), out=pT[:, sub, 16:528], in_=bank[:, :])
                                    if sub == 1:
                                        tk_pT.wrote(t_e)
                                else:
                                    t_e = A(guT[:, sub, :], bank[:, :], AF.Gelu_apprx_tanh, deps=[tl] + tk_gu[sub].wr())
                                    tk_gu[sub].wrote(t_e)
                                btk.read(t_e)
                        else:
                            bank, btk = nbig()
                            for tb in range(4):
                                tl = None
                                for kc in range(8):
                                    tl = MM(bank[:, tb * 256:(tb + 1) * 256], lhsT=hT[:, kc, tb * 128:(tb + 1) * 128], rhs=wv[:, kc, :],
                                            start=(kc == 0), stop=(kc == 7),
                                            deps=[t_w, hT_w[kc]] + (btk.wr() if (kc == 0 and tb == 0) else []), inc=(kc == 7))
                                tl_u = tl
                                if u == 4 and tb == 0:
                                    chk(60, [tl])
                                if tb % 2 == 0:
                                    continue
                                if u == 4 and tb == 1:
                                    chk(62, [tl])
                                for tb2 in (tb - 1, tb):
                                    if u in (4, 5):
                                        half = u - 4
                                        t_e = evac(Vw[:, 4 + tb2, half * 256:(half + 1) * 256], bank[:, tb2 * 256:(tb2 + 1) * 256],
                                                   [tl] + tk_Vc[tb2].wr(), eng=("act" if (tb // 2 + u) % 2 == 0 else "dve"))
                                        if half == 0:
                                            v_ev[tb2] = t_e
                                        else:
                                            tk_Vc[tb2].wrote([v_ev[tb2], t_e])
                                        v_rd.append(t_e)
                                    else:
                                        gv_tok[tb2] = (bank, btk, tl)
                                if u == 4 and tb == 1:
                                    chk(61, [tl] + v_rd)
                            btk.wrote(tl_u)
                            for t_ in v_rd:
                                btk.read(t_)
                        tk_hT.read(tl_u)
                        w_done(wi, tl_u)
                        chk(50 + u, [tl_u, tk_qT[0].w, tk_qT[1].w, tk_qT[2].w, tk_qT[3].w] + [tk_Kc[c_].w for c_ in range(4)] + [tk_Vc[c_].w for c_ in range(4)] + [tk_pT.w, tk_gu[0].w, tk_gu[1].w])

                    chk(5, [tk_qT[c].w for c in range(4)] + [tk_Kc[c].w for c in range(4)] + [tk_Vc[c].w for c in range(4)] + [tk_pT.w, tk_gu[0].w, tk_gu[1].w, tl_u])
                    if tile < ntiles - 1:
                        dk = [tk_Kc[c].w for c in range(4)]
                        dv = [tk_Vc[tb].w for tb in range(4)]
                        t1 = POOL.dma(lambda e, l=l: e.dma_start(out=kscr[l], in_=KTw[:, :, 512:1024]), kv_sem, deps=dk)
                        t2 = POOL.dma(lambda e, l=l: e.dma_start(out=vscr[l], in_=Vw[:, 4:8, :]), kv_sem, deps=dv)
                        t3 = POOL.dma(lambda e, l=l: e.dma_start(out=pscr[l], in_=pT[:, :, 512:528]), kv_sem, deps=[tk_pT.w])
                        t_sv = (kv_sem.sem, kv_sem.cnt)
                        save_tok[l] = [t_sv]
                        for c in range(4):
                            tk_Kc[c].read(t_sv)
                            tk_Vc[c].read(t_sv)
                        tk_pT.read(t_sv)

                    t_vs = {}
                    t_gs = {}
                    for tb in range(4):
                        bank, btk, tlv = gv_tok[tb]
                        t_g = A(gv[:, tb, :], bank[:, tb * 256:(tb + 1) * 256], AF.Gelu_apprx_tanh, deps=[tlv] + tk_gv[tb].wr(),
                                accum_out=lnst[:, tb, 0:1])
                        btk.read(t_g)
                        t_gs[tb] = t_g
                    sg_tmp = {}

                    def sg_p1(tb):
                        k = tb % 2
                        t_m = op(DVE, "tensor_scalar", [t_gs[tb]], out=lnst[:, tb, 1:2], in0=lnst[:, tb, 0:1], scalar1=-1.0 / 256, scalar2=None,
                                 op0=ALU.mult)
                        t_c = op(DVE, "tensor_scalar", [t_m] + tk_gvc[k].wr(), out=gvc[:, k, :], in0=gv[:, tb, :], scalar1=lnst[:, tb, 1:2], scalar2=None,
                                 op0=ALU.add)
                        tk_gvc[k].wrote(t_c)
                        sg_tmp[tb] = t_c

                    def sg_p2(tb):
                        k = tb % 2
                        t_s = A(gv[:, tb, :], gvc[:, k, :], AF.Square, deps=[sg_tmp[tb]], accum_out=lnst[:, tb, 2:3])
                        t_q0 = A(lnst[:, tb, 3:4], lnst[:, tb, 2:3], AF.Ln, deps=[t_s], scale=1.0 / 256, bias=EPS)
                        t_q = A(lnst[:, tb, 4:5], lnst[:, tb, 3:4], AF.Exp, deps=[t_q0], scale=-0.5)
                        tk_gvc[k].read(t_s)
                        sg_tmp[tb] = t_q

                    def sg_p3(tb):
                        k = tb % 2
                        t_v = op(DVE, "scalar_tensor_tensor", [sg_tmp[tb]] + tk_vn[tb].wr(), out=vn[:, tb, :], in0=gvc[:, k, :],
                                 scalar=lnst[:, tb, 4:5], in1=bc[:, l * 256:(l + 1) * 256], op0=ALU.mult, op1=ALU.mult)
                        tk_gvc[k].read(t_v)
                        tk_gv[tb].wrote(t_v)
                        tk_vn[tb].wrote(t_v)
                        t_vs[tb] = t_v

                    sg_sched = {}
                    for tb in range(4):
                        sg_sched[3 + 7 * tb] = (sg_p1, tb)
                        sg_sched[4 + 7 * tb] = (sg_p2, tb)
                        sg_sched[5 + 7 * tb] = (sg_p3, tb)

                    def sg_mix():
                        gbank = [nsmall(), nsmall()]
                        t_gl = None
                        for tb in range(4):
                            tl = None
                            for h in range(4):
                                gb, gtk = gbank[h // 2]
                                hp = h % 2
                                tl = MM(gb[hp * 64:(hp + 1) * 64, tb * 128:(tb + 1) * 128], lhsT=vn[:, tb, h * 64:(h + 1) * 64],
                                        rhs=wsTb[:, l * 4 + h, :], deps=[t_vs[tb]] + (gtk.wr() if (tb == 0 and hp == 0) else []), inc=(h == 3))
                            tk_vn[tb].read(tl)
                            t_gl = tl
                        for c in range(2):
                            gb, gtk = gbank[c]
                            gtk.wrote(t_gl)
                            bsb_ap = bass.AP(bc.tensor if hasattr(bc, "tensor") else bc, 1024 + (l * 2 + c) * 128, [[2048, 128], [0, 4], [1, 128]])
                            t1 = op(DVE, "tensor_tensor", [t_gl] + tk_br[6 + c].wr() + tk_U.wr(), out=brT[:, 6 + c, :].rearrange("p (t n) -> p t n", t=4),
                                    in0=gb[:, :].rearrange("p (t n) -> p t n", t=4), in1=bsb_ap, op=ALU.add)
                            gtk.read(t1)
                            t2 = op(DVE, "tensor_tensor", [t1, tk_gu[c].w], out=brT[:, 6 + c, :], in0=brT[:, 6 + c, :], in1=guT[:, c, :], op=ALU.mult)
                            tk_gu[c].read(t2)
                            tk_br[6 + c].wrote(t2)

                    dpp = [tk_pT.w, tk_ph.w]
                    t_l1 = op(DVE, "tensor_tensor", dpp + tk_pool["L1"].wr(), out=L1[:, :, 1:528], in0=pT[:, :, 1:528], in1=pT[:, :, 0:527], op=ALU.add)
                    t_l2a = op(DVE, "tensor_tensor", [t_l1] + tk_pool["L2"].wr(), out=L2[64:128, 0, 3:528], in0=L1[64:128, 0, 3:528], in1=L1[64:128, 0, 1:526], op=ALU.add)
                    t_l2b = op(DVE, "tensor_tensor", [t_l1, t_l2a], out=L2[:, 1, 3:528], in0=L1[:, 1, 3:528], in1=L1[:, 1, 1:526], op=ALU.add)
                    t_l3 = op(DVE, "tensor_tensor", [t_l2b], out=L1[:, 1, 7:528], in0=L2[:, 1, 7:528], in1=L2[:, 1, 3:524], op=ALU.add)
                    t_l4 = op(DVE, "tensor_tensor", [t_l3], out=L2[64:128, 1, 15:528], in0=L1[64:128, 1, 15:528], in1=L1[64:128, 1, 7:520], op=ALU.add)
                    fin = [(L1, 0, 0, 2), (L2, 1, 0, 4), (L1, 0, 1, 8), (L2, 1, 1, 16)]
                    tp = None
                    for (Lb, hp, c, w) in fin:
                        pr = slice(hp * 64, hp * 64 + 64)
                        tp = op(DVE, "scalar_tensor_tensor", [t_l4] + tk_pool["pooled"].wr(), out=pooledT[pr, c, :], in0=Lb[pr, c, 16:528],
                                scalar=1.0 / w, in1=pT[pr, c, 16:528], op0=ALU.mult, op1=ALU.subtract)
                        if first_tile:
                            t_a = op(DVE, "tensor_tensor", [t_l4, tp], out=Lb[pr, c, 16:32], in0=Lb[pr, c, 16:32], in1=pcorr[pr, c, :], op=ALU.mult)
                            tp = op(DVE, "tensor_tensor", [t_a, tp], out=pooledT[pr, c, 0:16], in0=Lb[pr, c, 16:32], in1=pT[pr, c, 16:32], op=ALU.subtract)
                    tk_pool["L1"].wrote(tp); tk_pool["L2"].wrote(tp); tk_pool["pooled"].wrote(tp)
                    tk_pT.read(tp); tk_ph.read(tp)

                    def pool_mix():
                        for c in range(2):
                            bank, btk = nsmall()
                            MM(bank[0:64, :], lhsT=wpb[0:64, l * 2 + c, :], rhs=pooledT[0:64, c, :], deps=[tp] + btk.wr())
                            tl = MM(bank[64:128, :], lhsT=wpb[64:128, l * 2 + c, :], rhs=pooledT[64:128, c, :], deps=[tp], inc=True)
                            btk.wrote(tl)
                            tk_pool["pooled"].read(tl)
                            t_e = A(brT[:, 4 + c, :], bank[:, :], AF.Identity, deps=[tl] + tk_br[4 + c].wr() + tk_U.wr(),
                                    scale=pp[:, PP_PS + l * 2 + c:PP_PS + l * 2 + c + 1])
                            btk.read(t_e)
                            tk_br[4 + c].wrote(t_e)

                    def branch_norm(c0, ncn, nfeat):
                        r, t_r = rms_stat([brT[:, c0 + i, :] for i in range(ncn)], nfeat, [tk_br[c0 + i].w for i in range(ncn)])
                        for i in range(ncn):
                            c = c0 + i
                            t_m = op(DVE, "scalar_tensor_tensor", [t_r, tk_br[c].w] + tk_mT[c].wr(), out=mT[:, c, :], in0=brT[:, c, :],
                                     scalar=pp[:, PP_MN + l * 8 + c:PP_MN + l * 8 + c + 1], in1=rstd[:, r, :], op0=ALU.mult, op1=ALU.mult)
                            tk_rstd[r].read(t_m)
                            tk_br[c].read(t_m)
                            tk_mT[c].wrote(t_m)
                            tk_U.read(t_m)

                    nhb = 32
                    st = [dict() for _ in range(nhb)]
                    abank = {}

                    def col0_of(qb):
                        return max(0, 128 * (4 - qb)) if first_tile else 0

                    def S0(g0):
                        hd = []
                        for g in (g0, g0 + 1):
                            qb, h = g // 8, g % 8
                            hp, c = h % 2, h // 2
                            zb, ztk = nbig()
                            hd.append(dict(g=g, qb=qb, h=h, hp=hp, c=c, pr=slice(hp * 64, hp * 64 + 64), zb=zb, ztk=ztk,
                                           q_ap=qT[slice(hp * 64, hp * 64 + 64), c, qb * 128:(qb + 1) * 128], first=True))
                        qb = hd[0]["qb"]
                        col0 = col0_of(qb)
                        if col0 == 0:
                            regions = [(0, 128, True), (128, 512, False), (512, 640, True)]
                        elif col0 < 512:
                            regions = [(col0, 512, False), (512, 640, True)]
                        else:
                            regions = [(512, 640, True)]
                        for (r0, r1, masked) in regions:
                            for H in hd:
                                c = H["c"]
                                d = ([tk_qT[c].w, tk_Kc[c].w, tk_Kh.w] + H["ztk"].wr()) if H["first"] else []
                                H["first"] = False
                                H["tl"] = MM(H["zb"][:, r0:r1], lhsT=H["q_ap"], rhs=KTw[H["pr"], c, qb * 128 + r0:qb * 128 + r1], start=True, stop=(not masked), deps=d)
                            if masked:
                                for H in hd:
                                    mcol = 0 if r0 == 0 else 128
                                    H["tl"] = MM(H["zb"][:, r0:r1], lhsT=identb[:], rhs=negmb[:, mcol:mcol + 128], start=False, stop=True, inc=(r0 == 512))
                        for H in hd:
                            tl, c = H["tl"], H["c"]
                            H["ztk"].wrote(tl)
                            tk_qT[c].read(tl); tk_Kc[c].read(tl); tk_Kh.read(tl)
                            st[H["g"]].update(zb=H["zb"], ztk=H["ztk"], tz=tl, col0=col0, qb=H["qb"], h=H["h"], hp=H["hp"], c=c)

                    def S1(g):
                        s = st[g]
                        k3 = g % 3
                        sl = slice(s["col0"], 640)
                        Te, Tsp = tk_att["e"][k3], tk_att["sp"][k3]
                        t_e = A(e_buf[k3][:, sl], s["zb"][:, sl], AF.Exp, deps=[s["tz"]] + Te.wr())
                        s["ztk"].read(t_e)
                        Te.wrote(t_e)
                        t_sp = A(sp_buf[k3][:, sl], e_buf[k3][:, sl], AF.Ln, deps=[t_e] + Tsp.wr(), bias=1.0)
                        Tsp.wrote(t_sp)

                    def S2(g):
                        s = st[g]
                        k, k3 = g % 2, g % 3
                        col0 = s["col0"]
                        sl = slice(col0, 640)
                        Tsp, TC = tk_att["sp"][k3], tk_att["C"][k]
                        dz = []
                        if col0 > 0:
                            dz = [op(DVE, "memset", TC.wr(), C_b[k][:, col0:col0 + 1], 0.0)]
                        t_c = op(DVE, "tensor_tensor_scan", [Tsp.w] + TC.wr() + dz, out=C_b[k][:, col0 + 1:641], data0=ones640[:, sl],
                                 data1=sp_buf[k3][:, sl], initial=0.0, op0=ALU.mult, op1=ALU.subtract)
                        TC.wrote(t_c)
                        Tsp.read(t_c)

                    def S2b(g):
                        s = st[g]
                        k, k3 = g % 2, g % 3
                        col0 = s["col0"]
                        sl = slice(col0, 640)
                        Te, Tsp, Tw, TC = tk_att["e"][k3], tk_att["sp"][k3], tk_att["w"][k], tk_att["C"][k]
                        t_t = A(sp_buf[k3][:, sl], C_b[k][:, col0:640], AF.Exp, deps=[TC.w] + Tsp.wr(), scale=-1.0, bias=C_b[k][:, 640:641])
                        TC.read(t_t)
                        Tsp.wrote(t_t)
                        t_w = op(DVE, "tensor_tensor", [Tsp.w, Te.w] + Tw.wr(), out=w_b[k][:, sl], in0=e_buf[k3][:, sl], in1=sp_buf[k3][:, sl], op=ALU.mult)
                        Te.read(t_w)
                        Tsp.read(t_w)
                        Tw.wrote(t_w)

                    def S3(g):
                        s = st[g]
                        k = g % 2
                        k3 = g % 3
                        Tw, TwT = tk_att["w"][k], tk_att["wT"][k3]
                        col0 = s["col0"]
                        tl = None
                        first = True
                        for kc in range(col0 // 128, 5):
                            tl = TR(psT[:, kc * 128:(kc + 1) * 128], w_b[k][:, kc * 128:(kc + 1) * 128], identb[:],
                                    deps=[Tw.w] + (psT_tk.wr() if first else []), inc=(kc == 4))
                            first = False
                        psT_tk.wrote(tl)
                        Tw.read(tl)
                        t_cp = evac(wT_b[k3][:, col0:640], psT[:, col0:640], [tl] + TwT.wr(), eng=("act" if g % 2 else "dve"))
                        psT_tk.read(t_cp)
                        TwT.wrote(t_cp)

                    def S4(g0):
                        s0 = st[g0]
                        qb, col0, c = s0["qb"], s0["col0"], s0["c"]
                        if s0["h"] == 0:
                            abank[qb] = aT_bank
                        ab, atk = abank[qb]
                        kcs = list(range(col0 // 128, 5))
                        tls = {}
                        for i, kc in enumerate(kcs):
                            blk = qb + kc
                            dv = [tk_Vh.w] if blk < 4 else [tk_Vc[blk - 4].w]
                            for g in (g0, g0 + 1):
                                h, hp = st[g]["h"], st[g]["hp"]
                                TwT = tk_att["wT"][g % 3]
                                tls[g] = MM(ab[hp * 64:(hp + 1) * 64, c * 128:(c + 1) * 128], lhsT=Vw[:, blk, h * 64:(h + 1) * 64],
                                            rhs=wT_b[g % 3][:, kc * 128:(kc + 1) * 128], start=(i == 0), stop=(i == len(kcs) - 1),
                                            deps=[TwT.w] + dv + (atk.wr() if (h == 0 and i == 0) else []), inc=(i == len(kcs) - 1))
                        for g in (g0, g0 + 1):
                            tk_att["wT"][g % 3].read(tls[g])
                        tl = tls[g0 + 1]
                        for blk in range(qb + kcs[0], qb + 5):
                            (tk_Vh if blk < 4 else tk_Vc[blk - 4]).read(tl)
                        if st[g0 + 1]["h"] == 7:
                            atk.wrote(tl)
                            t_e = evac(brT[:, 0:4, qb * 128:(qb + 1) * 128], ab[:, :].rearrange("p (c n) -> p c n", c=4),
                                       [tl] + tk_br[0].wr() + tk_U.wr())
                            atk.read(t_e)
                            st[g0 + 1]["t_out"] = t_e

                    for step in range(nhb + 6):
                        for si, fn in ((2, S2), (3, S2b), (1, S1), (4, S3)):
                            g = step - si
                            if 0 <= g < nhb:
                                fn(g)
                        g = step - 5
                        if 0 <= g < nhb and g % 2 == 1:
                            S4(g - 1)
                        if step < nhb and step % 2 == 0:
                            S0(step)
                        if step in sg_sched:
                            sg_sched[step][0](sg_sched[step][1])
                        if step == 28:
                            sg_mix()
                            pool_mix()
                        if step == 30:
                            branch_norm(4, 2, 256)
                        if step == 32:
                            branch_norm(6, 2, 256)
                    t_attn = [st[qb * 8 + 7]["t_out"] for qb in range(4)]
                    for c in range(4):
                        tk_br[c].wrote(t_attn)

                    chk(8, [tk_br[c].w for c in range(8)])
                    branch_norm(0, 4, 512)

                    chk(10, [tk_mT[c].w for c in range(8)])
                    for j in range(4):
                        ws, t_w, wi = w_get()
                        wv = ws.rearrange("p (kc n) -> p kc n", kc=8)
                        tl = None
                        for sub in range(2):
                            o = j * 2 + sub
                            bank, btk = nsmall()
                            for ki, kc in enumerate((4, 5, 6, 7, 0, 1, 2, 3)):
                                tl = MM(bank[:, :], lhsT=wv[:, kc, sub * 128:(sub + 1) * 128], rhs=mT[:, kc, :], start=(ki == 0), stop=(ki == 7),
                                        deps=[t_w, tk_mT[kc].w] + (btk.wr() if ki == 0 else []), inc=(ki == 7))
                            btk.wrote(tl)
                            t_x2 = op(DVE, "scalar_tensor_tensor", [tl] + tk_xT[o].wr(), out=xT[:, o, :], in0=bank[:, :],
                                      scalar=modT[:, l, 16 + o:17 + o], in1=xT[:, o, :], op0=ALU.mult, op1=ALU.add)
                            btk.read(t_x2)
                            tk_xT[o].wrote(t_x2)
                        for kc in range(8):
                            tk_mT[kc].read(tl)
                        w_done(wi, tl)

                    chk(11, [tk_xT[c].w for c in range(8)])
                    norm_mod(l, lambda c: A2[:, l, c:c + 1], lambda c: modT[:, l, 24 + c:25 + c])

                    for f in range(NF):
                        ws, t_w, wi = w_get()
                        wv = ws.rearrange("p (kc g n) -> p kc g n", kc=8, g=2)
                        bank, btk = nbig()
                        tl = None
                        for g_ in range(2):
                            for kc in range(8):
                                tl = MM(bank[:, g_ * 512:(g_ + 1) * 512], lhsT=wv[:, kc, g_, :], rhs=hT[:, kc, :], start=(kc == 0), stop=(kc == 7),
                                        deps=[t_w, hT_w[kc]] + (btk.wr() if (kc == 0 and g_ == 0) else []), inc=(kc == 7 and g_ == 1))
                        btk.wrote(tl)
                        w_done(wi, tl)
                        k = f % 2
                        t_s = A(silu[:, k, :], bank[:, 0:512], AF.Silu, deps=[tl] + tk_silu[k].wr())
                        t_a = op(DVE, "tensor_tensor", [t_s] + tk_act[f].wr() + (tk_U.wr() if f == 0 else []), out=actT[:, f, :], in0=silu[:, k, :],
                                 in1=bank[:, 512:1024], op=ALU.mult)
                        btk.read(t_a)
                        tk_silu[k].wrote(t_s)
                        tk_silu[k].read(t_a)
                        tk_act[f].wrote(t_a)
                        t_gu_last = tl
                    tk_hT.read(t_gu_last)

                    chk(12, [tk_act[f].w for f in range(NF)])
                    t_dn = None
                    for o in range(8):
                        bank, btk = nsmall()
                        tl = None
                        for half in range(2):
                            ws, t_w, wi = w_get()
                            wv = ws[:, 0:1408].rearrange("p (f n) -> p f n", f=11)
                            for fi in range(11):
                                f = half * 11 + fi
                                tl = MM(bank[:, :], lhsT=wv[:, fi, :], rhs=actT[:, f, :], start=(f == 0), stop=(f == NF - 1),
                                        deps=[t_w, tk_act[f].w] + (btk.wr() if f == 0 else []), inc=(fi == 10))
                            w_done(wi, tl)
                        btk.wrote(tl)
                        t_x2 = op(DVE, "scalar_tensor_tensor", [tl] + tk_xT[o].wr(), out=xT[:, o, :], in0=bank[:, :],
                                  scalar=modT[:, l, 40 + o:41 + o], in1=xT[:, o, :], op0=ALU.mult, op1=ALU.add)
                        btk.read(t_x2)
                        tk_xT[o].wrote(t_x2)
                        t_dn = tl
                    for f in range(NF):
                        tk_act[f].read(t_dn)
                    tk_U.wrote(t_dn)
                    for c in range(8):
                        tk_br[c].wrote(t_dn)

                chk(13, [tk_xT[c].w for c in range(8)])
                r, t_r = rms_stat([xT[:, c, :] for c in range(8)], D, [tk_xT[c].w for c in range(8)])
                for cg in range(2):
                    t_xo = None
                    for i in range(4):
                        c = cg * 4 + i
                        t_xo = op(DVE, "scalar_tensor_tensor", [t_r, tk_xT[c].w] + tk_U.wr(), out=xo[:, i, :], in0=xT[:, c, :],
                                  scalar=pp[:, PP_FN + c:PP_FN + c + 1], in1=rstd[:, r, :], op0=ALU.mult, op1=ALU.mult)
                        tk_xT[c].read(t_xo)
                    tk_rstd[r].read(t_xo)
                    t_ev = None
                    for tb in range(4):
                        bank, btk = nsmall()
                        tl = None
                        for i in range(4):
                            tl = TR(bank[:, i * 128:(i + 1) * 128], xo[:, i, tb * 128:(tb + 1) * 128], ident32[:],
                                    deps=[t_xo] + (btk.wr() if i == 0 else []), inc=(i == 3))
                        btk.wrote(tl)
                        t_ev = evac(xout[:, tb, :], bank[:, :], [tl] + tk_U.wr())
                        btk.read(t_ev)
                        t_ev_all = t_ev
                        st_tb = POOL.dma(lambda e, tok0=tok0, tb=tb, cg=cg: e.dma_start(
                            out=out_d[tok0 + tb * 128:tok0 + (tb + 1) * 128, cg * 512:(cg + 1) * 512], in_=xout[:, tb, :]), o_st, deps=[t_ev])
                        out_toks.append(st_tb)
                    tk_U.wrote((o_st.sem, o_st.cnt))
                    tk_U.read(tl)

        except _Stop as ex:
            return finish(list(ex.args[0]))
        POOL.final_wait([(o_st.sem, o_st.cnt)])
        SP.final_wait([(o_st.sem, o_st.cnt)])
        S.run()
    return nc


_CACHE = {}


def _host_consts():
    ident = np.eye(128, dtype=np.float32)
    a = np.arange(128)[:, None]
    b = np.arange(640)[None, :]
    maskA = ((b >= a) & (b < a + 512)).astype(np.float32)
    s = np.arange(128)[:, None]
    t = np.arange(128)[None, :]
    maskS = (s <= t).astype(np.float32)
    pc = np.zeros((128, 2, 16), np.float32)
    wins = [2, 4, 8, 16]
    tt = np.arange(16)
    for g in range(4):
        hp, c = g % 2, g // 2
        pc[hp * 64:(hp + 1) * 64, c, :] = 1.0 / np.minimum(tt + 1, wins[g])
    return np.concatenate([ident, maskA, maskS, pc.reshape(128, 32)], axis=1)


def _fm(v, nchunk):
    v = np.asarray(v, np.float32)
    lead = v.shape[:-1]
    r = v.reshape(*lead, nchunk, 128)
    r = np.moveaxis(r, -1, 0)
    return r.reshape(128, -1)


def prep_shared(inputs, depth=DEPTH):
    g = lambda k: np.asarray(inputs[k], np.float32)
    bada = _fm(g("b_ada"), 48)
    nmi = _fm(g("norm_mix_in"), 8)
    nfi = _fm(g("norm_ffn_in"), 8)
    fn = _fm(g("final_norm"), 8)
    mn = _fm(g("mix_norm"), 8)
    psc = _fm(g("pool_scale"), 2)
    sgn = np.broadcast_to(g("sg_norm").reshape(1, DEPTH * 256), (128, DEPTH * 256))
    bs = g("b_s")
    bsb = np.zeros((128, DEPTH, 2, 128), np.float32)
    for l in range(DEPTH):
        for c in range(2):
            bsb[0:64, l, c, :] = bs[l, 2 * c][None, :]
            bsb[64:128, l, c, :] = bs[l, 2 * c + 1][None, :]
    bc = np.concatenate([sgn, bsb.reshape(128, DEPTH * 256)], axis=1)
    ws = g("w_s")
    wsT = np.ascontiguousarray(np.transpose(ws, (3, 0, 1, 2))).reshape(128, DEPTH * 4 * 128)
    wpool = g("w_pool")
    wp = np.zeros((128, DEPTH, 2, 64), np.float32)
    for l in range(DEPTH):
        for gi in range(4):
            hp, i = gi % 2, gi // 2
            wp[hp * 64:(hp + 1) * 64, l, i, :] = wpool[l, gi]
    shared = dict(bc=np.ascontiguousarray(bc), cst=_host_consts(), wsT=wsT, wp=wp.reshape(128, 512),
                  w_ada=g("w_ada"), w_in=g("w_in"), w_out=g("w_out"), w_gate_up=g("w_gate_up"), w_down=g("w_down"))
    pp_common = np.concatenate([bada, nmi, nfi, fn, mn, psc], axis=1)
    return shared, pp_common


def kernel(x, c, w_ada, b_ada, norm_mix_in, w_in, w_pool, pool_scale, sg_norm, w_s, b_s,
           mix_norm, w_out, norm_ffn_in, w_gate_up, w_down, final_norm):
    inputs = dict(x=x, c=c, w_ada=w_ada, b_ada=b_ada, norm_mix_in=norm_mix_in, w_in=w_in, w_pool=w_pool,
                  pool_scale=pool_scale, sg_norm=sg_norm, w_s=w_s, b_s=b_s, mix_norm=mix_norm, w_out=w_out,
                  norm_ffn_in=norm_ffn_in, w_gate_up=w_gate_up, w_down=w_down, final_norm=final_norm)
    x = np.asarray(x, np.float32)
    c = np.asarray(c, np.float32)
    B = x.shape[0]
    if "nc" not in _CACHE:
        _CACHE["nc"] = build_program()
    nc = _CACHE["nc"]
    shared, pp_common = prep_shared(inputs)
    in_maps = []
    for b in range(B):
        m = dict(shared)
        m["x"] = np.ascontiguousarray(x[b])
        m["pp"] = np.ascontiguousarray(np.concatenate([_fm(c[b], 8), pp_common], axis=1))
        in_maps.append(m)
    res = run_bass_kernel_spmd(nc, in_maps, core_ids=list(range(B)))
    return np.stack([np.asarray(r["out"], np.float32) for r in res.results], axis=0)
```

```python
import numpy as np
from contextlib import ExitStack
import concourse.bass as bass
import concourse.mybir as mybir
from concourse.bass_utils import run_bass_kernel_spmd

F32 = mybir.dt.float32
BF16 = mybir.dt.bfloat16
AF = mybir.ActivationFunctionType
ALU = mybir.AluOpType

D = 1024
SEQ = 8192
DEPTH = 4
TT = 512
NKC = 8
NF = 22
DFF = 2816
NU = 51
NSLOT = 8
EPS = 1e-6
SEM_LIMIT = 6000
SAME_ENGINE_SYNC = True


class Eng:
    def __init__(self, sched, name):
        self.s = sched
        self.name = name
        self.ops = []
        self.sem = None
        self.cnt = 0
        self.nsem = 0
        self.waited = {}

    def _cur_sem(self):
        if self.sem is None or self.cnt >= SEM_LIMIT:
            self.sem = self.s.new_sem(f"{self.name}{self.nsem}")
            self.nsem += 1
            self.cnt = 0
        return self.sem

    def _waits(self, deps):
        best = {}

        def add(d):
            if d is None:
                return
            if isinstance(d, list):
                for dd in d:
                    add(dd)
                return
            k = d[0].name
            if k not in best or d[1] > best[k][1]:
                best[k] = d

        for d in deps:
            add(d)
        waits = []
        for k, (sem, val) in best.items():
            if self.waited.get(k, 0) >= val:
                continue
            self.waited[k] = val
            waits.append((sem, val))
        return waits

    def emit(self, fn, deps=(), inc=True):
        waits = self._waits(deps)
        tok = None
        incinfo = None
        if inc:
            sem = self._cur_sem()
            self.cnt += 1
            tok = (sem, self.cnt)
            incinfo = (sem, 1)
            if not SAME_ENGINE_SYNC:
                self.waited[sem.name] = self.cnt
        self.ops.append((waits, fn, incinfo))
        return tok

    def dma(self, fn, dsem, deps=()):
        waits = self._waits(deps)
        dsem.cnt += 16
        tok = (dsem.sem, dsem.cnt)
        self.ops.append((waits, fn, (dsem.sem, 16)))
        return tok

    def final_wait(self, toks):
        self.ops.append((self._waits(toks), None, None))

    def replay(self, e):
        for waits, fn, incinfo in self.ops:
            for sem, val in waits:
                e.wait_ge(sem, val)
            if fn is None:
                continue
            ins = fn(e)
            if incinfo is not None:
                ins.then_inc(incinfo[0], incinfo[1])
        self.ops = []


class DmaSem:
    def __init__(self, sched, name):
        self.sem = sched.new_sem(name)
        self.cnt = 0


class Sched:
    def __init__(self, nc, es):
        self.nc = nc
        self.es = es
        self.nsems = 0
        self.pe = Eng(self, "pe")
        self.act = Eng(self, "act")
        self.dve = Eng(self, "dve")
        self.pool = Eng(self, "pool")
        self.sp = Eng(self, "sp")

    def new_sem(self, name):
        self.nsems += 1
        return self.es.enter_context(self.nc.semaphore(f"s_{name}"))

    def dsem(self, name):
        return DmaSem(self, name)

    def sb(self, name, shape, dtype):
        return self.es.enter_context(self.nc.sbuf_tensor("sb_" + name, list(shape), dtype))

    def ps(self, name, shape, dtype=F32):
        return self.es.enter_context(self.nc.psum_tensor("pm_" + name, list(shape), dtype))

    def run(self):
        block = self.es.enter_context(self.nc.Block())

        @block.tensor
        def _(e):
            self.pe.replay(e)

        @block.scalar
        def _(e):
            self.act.replay(e)

        @block.vector
        def _(e):
            self.dve.replay(e)

        @block.gpsimd
        def _(e):
            self.pool.replay(e)

        @block.sync
        def _(e):
            self.sp.replay(e)


class Tk:
    def __init__(self):
        self.w = None
        self.r = []

    def rd(self):
        return [self.w]

    def wr(self):
        return [self.w] + self.r

    def wrote(self, tok):
        self.w = tok
        self.r = []

    def read(self, tok):
        if tok is not None:
            self.r.append(tok)
            if len(self.r) > 24:
                self.r = self.r[-24:]


def build_program(ntiles=SEQ // TT, depth=DEPTH, stage=9):
    nc = bass.Bass("TRN2", target_bir_lowering=False)
    ntok = ntiles * TT
    x_d = nc.dram_tensor("x", [ntok, D], F32, kind="ExternalInput").ap()
    out_d = nc.dram_tensor("out", [ntok, D], F32, kind="ExternalOutput").ap()
    pp_d = nc.dram_tensor("pp", [128, 312], F32, kind="ExternalInput").ap()
    bc_d = nc.dram_tensor("bc", [128, 2048], F32, kind="ExternalInput").ap()
    cst_d = nc.dram_tensor("cst", [128, 128 + 640 + 128 + 32], F32, kind="ExternalInput").ap()
    wsT_d = nc.dram_tensor("wsT", [128, 2048], F32, kind="ExternalInput").ap()
    wp_d = nc.dram_tensor("wp", [128, 512], F32, kind="ExternalInput").ap()
    wada_d = nc.dram_tensor("w_ada", [DEPTH, D, 6 * D], F32, kind="ExternalInput").ap()
    win_d = nc.dram_tensor("w_in", [DEPTH, D, 2304], F32, kind="ExternalInput").ap()
    wout_d = nc.dram_tensor("w_out", [DEPTH, D, D], F32, kind="ExternalInput").ap()
    wgu_d = nc.dram_tensor("w_gate_up", [DEPTH, D, 2 * DFF], F32, kind="ExternalInput").ap()
    wdn_d = nc.dram_tensor("w_down", [DEPTH, DFF, D], F32, kind="ExternalInput").ap()
    wscr = nc.dram_tensor("wscr", [DEPTH * NU, 128, 2048], BF16).ap()
    kscr = nc.dram_tensor("kscr", [DEPTH, 128, 4, 512], BF16).ap()
    vscr = nc.dram_tensor("vscr", [DEPTH, 128, 4, 512], BF16).ap()
    pscr = nc.dram_tensor("pscr", [DEPTH, 128, 2, 16], F32).ap()

    with ExitStack() as es:
        S = Sched(nc, es)
        PE, ACT, DVE, POOL, SP = S.pe, S.act, S.dve, S.pool, S.sp

        def op(eng, name, deps, *a, inc=True, **kw):
            return eng.emit(lambda e: getattr(e, name)(*a, **kw), deps, inc=inc)

        def MM(out, lhsT, rhs, start=True, stop=True, deps=(), inc=False):
            return PE.emit(lambda e: e.matmul(out, lhsT=lhsT, rhs=rhs, start=start, stop=stop), deps, inc=inc)

        def TR(out, in_, ident, deps=(), inc=False):
            return PE.emit(lambda e: e.transpose(out, in_, ident), deps, inc=inc)

        def A(out, in_, func, deps=(), **kw):
            return ACT.emit(lambda e: e.activation(out=out, in_=in_, func=func, **kw), deps)

        xT = S.sb("xT", [128, 8, TT], F32)
        hT = S.sb("hT", [128, 8, TT], BF16)
        mT = S.sb("mT", [128, 8, TT], BF16)
        U = S.sb("U", [128, 5632], F32)
        qT = S.sb("qT", [128, 4, TT], BF16)
        KTw = S.sb("KTw", [128, 4, 1024], BF16)
        Vw = S.sb("Vw", [128, 8, 512], BF16)
        pT = S.sb("pT", [128, 2, 528], F32)
        L1 = S.sb("L1", [128, 2, 528], F32)
        L2 = S.sb("L2", [128, 2, 528], F32)
        pooledT = S.sb("pooledT", [128, 2, TT], BF16)
        guT = S.sb("guT", [128, 2, TT], F32)
        vn = S.sb("vn", [128, 4, 256], BF16)
        gv = S.sb("gv", [128, 4, 256], F32)
        gvc = S.sb("gvc", [128, 2, 256], F32)
        lnst = S.sb("lnst", [128, 4, 8], F32)
        att = S.sb("att", [128, 6736], F32)
        sqb = S.sb("sqb", [128, 3, TT], BF16)
        rstd = S.sb("rstd", [128, 2, TT], F32)
        silu = S.sb("silu", [128, 2, TT], F32)
        wring = S.sb("wring", [128, NSLOT, 2048], BF16)
        negS = S.sb("negS", [128, 2], F32)
        ident32 = S.sb("ident32", [128, 128], F32)
        identb = S.sb("identb", [128, 128], BF16)
        maskA = S.sb("maskA", [128, 640], F32)
        maskS = S.sb("maskS", [128, 128], F32)
        pcorr = S.sb("pcorr", [128, 2, 16], F32)
        onesb = S.sb("onesb", [128, 128], BF16)
        ones640 = S.sb("ones640", [128, 640], BF16)
        pp = S.sb("pp", [128, 312], F32)
        bc = S.sb("bc", [128, 2048], F32)
        wsTb = S.sb("wsTb", [128, 16, 128], BF16)
        wpb = S.sb("wpb", [128, 8, 64], BF16)
        cact = S.sb("cact", [128, 8], F32)
        modT = S.sb("modT", [128, DEPTH, 48], F32)
        A1 = S.sb("A1", [128, DEPTH, 8], F32)
        A2 = S.sb("A2", [128, DEPTH, 8], F32)

        xin = U[:, 0:4096].rearrange("p (t n) -> p t n", t=4)
        brT = U[:, 0:4096].rearrange("p (c n) -> p c n", c=8)
        actT = U[:].bitcast(BF16).rearrange("p (f n) -> p f n", f=NF)
        xo = U[:, 0:2048].rearrange("p (c n) -> p c n", c=4)
        xout = U[:, 2048:4096].rearrange("p (t n) -> p t n", t=4)
        attb = att[:].bitcast(BF16)
        e_buf = [att[:, k * 640:(k + 1) * 640] for k in range(3)]
        sp_buf = [att[:, 1920 + k * 640:1920 + (k + 1) * 640] for k in range(3)]
        C_b = [att[:, 3840 + k * 648:3840 + (k + 1) * 648] for k in range(2)]
        w_b = [attb[:, 10272 + k * 640:10272 + (k + 1) * 640] for k in range(2)]
        wT_b = [attb[:, 11552 + k * 640:11552 + (k + 1) * 640] for k in range(3)]
        xin2 = att[:, 0:4096].rearrange("p (t n) -> p t n", t=4)
        st32 = [U[:, 0:2048], U[:, 2048:4096], att[:, 0:2048]]
        Ub = U[:].bitcast(BF16)
        stb = [Ub[:, 8192:10240], attb[:, 4096:6144], attb[:, 6144:8192]]
        negmb = S.sb("negmb", [128, 256], BF16)
        PP_C, PP_BADA, PP_NMI, PP_NFI, PP_FN, PP_MN, PP_PS = 0, 8, 200, 232, 264, 272, 304

        psA = S.ps("psA", [128, 1024])
        psB = S.ps("psB", [128, 1024])
        psT = S.ps("psT", [128, 1024], BF16)
        ps5 = S.ps("ps5", [128, 512])
        ps6 = S.ps("ps6", [128, 512])
        ps7 = S.ps("ps7", [128, 512])
        big = [(psA, Tk()), (psB, Tk())]
        small = [(ps6, Tk()), (ps7, Tk())]
        aT_bank = (ps5, Tk())
        psT_tk = Tk()
        cnt = {"big": 0, "small": 0, "ev": 0}

        def nbig():
            cnt["big"] += 1
            return big[cnt["big"] % 2]

        def nsmall():
            cnt["small"] += 1
            return small[cnt["small"] % 2]

        def evac(out, in_, deps, eng=None):
            if eng is None:
                cnt["ev"] += 1
                eng = "act" if cnt["ev"] % 2 else "dve"
            if eng == "act":
                return A(out, in_, AF.Identity, deps=deps)
            return op(DVE, "tensor_copy", deps, out=out, in_=in_)

        dl = S.dsem("cload")
        t_pp = SP.dma(lambda e: e.dma_start(out=pp[:], in_=pp_d), dl)
        t_bc = SP.dma(lambda e: e.dma_start(out=bc[:], in_=bc_d), dl)
        t_i32 = SP.dma(lambda e: e.dma_start(out=ident32[:], in_=cst_d[:, 0:128]), dl)
        t_mA = SP.dma(lambda e: e.dma_start(out=maskA[:], in_=cst_d[:, 128:768]), dl)
        t_mS = SP.dma(lambda e: e.dma_start(out=maskS[:], in_=cst_d[:, 768:896]), dl)
        t_pc = SP.dma(lambda e: e.dma_start(out=pcorr[:], in_=cst_d[:, 896:928].rearrange("p (c n) -> p c n", c=2)), dl)
        t_ws = SP.dma(lambda e: e.dma_start(out=st32[0], in_=wsT_d), dl)
        t_wp = SP.dma(lambda e: e.dma_start(out=st32[1][:, 0:512], in_=wp_d), dl)
        t_const = (dl.sem, dl.cnt)
        cd = [t_const]
        t_idb = op(DVE, "tensor_copy", cd, out=identb[:], in_=ident32[:])
        t_ones = op(DVE, "memset", [], onesb[:], 1.0)
        t_o640 = op(DVE, "memset", [], ones640[:], 1.0)
        t_nm0 = op(DVE, "tensor_scalar", cd, out=negmb[:, 0:128], in0=maskA[:, 0:128], scalar1=-1.0, scalar2=30000.0, op0=ALU.add, op1=ALU.mult)
        t_nm = op(DVE, "tensor_scalar", cd, out=negmb[:, 128:256], in0=maskA[:, 512:640], scalar1=-1.0, scalar2=30000.0, op0=ALU.add, op1=ALU.mult)
        t_wsb = None
        for i in range(16):
            t_wsb = op(DVE, "tensor_tensor", cd, out=wsTb[:, i, :], in0=st32[0][:, i * 128:(i + 1) * 128], in1=maskS[:], op=ALU.mult)
        t_wpb = op(DVE, "tensor_copy", cd, out=wpb[:].rearrange("p a b -> p (a b)"), in_=st32[1][:, 0:512])
        t_cact = A(cact[:], pp[:, PP_C:PP_C + 8], AF.Silu, deps=cd)
        st_free = [[t_wsb], [t_wpb], []]
        stb_free = [[], [], []]

        def finish(toks):
            for en in (PE, ACT, DVE, POOL, SP):
                en.final_wait(toks)
            S.run()
            return nc

        if stage == 0:
            return finish([t_const, t_idb, t_ones, t_o640, t_wsb, t_wpb, t_cact])
        ld_sem = [S.dsem(f"pl{i}") for i in range(3)]
        st_sem = [S.dsem(f"ps{i}") for i in range(3)]
        pcount = 0
        mod_ps = [ps5, ps6, ps7, psA]
        pst8 = {"pcount": 0, "t_mod_last": None, "tl": None}

        def mod_block(l, nb):
            mps = mod_ps[l % 4]
            s = pst8["pcount"] % 3
            pst8["pcount"] += 1
            src = wada_d[l].rearrange("(kc p) n -> p kc n", p=128)[:, :, nb * 256:(nb + 1) * 256]
            dst = st32[s].rearrange("p (kc n) -> p kc n", kc=8)
            t_ld = SP.dma(lambda e, dst=dst, src=src: e.dma_start(out=dst, in_=src), ld_sem[s], deps=st_free[s])
            tl = None
            for jj in range(2):
                j = nb * 2 + jj
                for kc in range(8):
                    tl = MM(mps[:, j:j + 1], lhsT=dst[:, kc, jj * 128:(jj + 1) * 128], rhs=cact[:, kc:kc + 1],
                            start=(kc == 0), stop=(kc == 7), deps=[t_ld, t_cact], inc=(jj == 1 and kc == 7))
            st_free[s] = [tl]
            if nb == 23:
                t_mod = op(DVE, "tensor_tensor", [tl, t_pp], out=modT[:, l, :], in0=mps[:, 0:48],
                           in1=pp[:, PP_BADA + l * 48:PP_BADA + (l + 1) * 48], op=ALU.add)
                t_a1 = op(DVE, "scalar_tensor_tensor", [t_mod], out=A1[:, l, :], in0=modT[:, l, 8:16], scalar=1.0,
                          in1=pp[:, PP_NMI + l * 8:PP_NMI + (l + 1) * 8], op0=ALU.add, op1=ALU.mult)
                pst8["t_mod_last"] = op(DVE, "scalar_tensor_tensor", [t_mod, t_a1], out=A2[:, l, :], in0=modT[:, l, 32:40], scalar=1.0,
                                        in1=pp[:, PP_NFI + l * 8:PP_NFI + (l + 1) * 8], op0=ALU.add, op1=ALU.mult)

        def unit_src(l, u):
            if u < 9:
                return win_d[l].rearrange("(kc p) n -> p kc n", p=128)[:, :, u * 256:(u + 1) * 256], "p (kc n) -> p kc n", dict(kc=8), 2048
            if u < 13:
                j = u - 9
                return wout_d[l].rearrange("(kc p) n -> p kc n", p=128)[:, :, j * 256:(j + 1) * 256], "p (kc n) -> p kc n", dict(kc=8), 2048
            if u < 35:
                f = u - 13
                return (wgu_d[l].rearrange("(kc p) (g f n) -> p kc g f n", p=128, g=2, n=128)[:, :, :, f, :],
                        "p (kc g n) -> p kc g n", dict(kc=8, g=2), 2048)
            k = u - 35
            o, half = k // 2, k % 2
            return (wdn_d[l].rearrange("(f p) n -> p f n", p=128)[:, half * 11:(half + 1) * 11, o * 128:(o + 1) * 128],
                    "p (f n) -> p f n", dict(f=11), 1408)

        pst9 = {"cast_rr": 0}

        def conv_unit(l, u):
            s = pst8["pcount"] % 3
            pst8["pcount"] += 1
            src, pat, kw, n = unit_src(l, u)
            dst = st32[s][:, 0:n].rearrange(pat, **kw)
            if 13 <= u < 35:
                for g_ in range(2):
                    t_ld = SP.dma(lambda e, dst=dst[:, :, g_, :], src=src[:, :, g_, :]: e.dma_start(out=dst, in_=src), ld_sem[s], deps=st_free[s])
            else:
                t_ld = SP.dma(lambda e, dst=dst, src=src: e.dma_start(out=dst, in_=src), ld_sem[s], deps=st_free[s])
            pst9["cast_rr"] += 1
            ce = [ACT, DVE, POOL][pst9["cast_rr"] % 3]
            if ce is ACT:
                t_c = A(stb[s][:, 0:n], st32[s][:, 0:n], AF.Identity, deps=[t_ld] + stb_free[s])
            else:
                t_c = op(ce, "tensor_copy", [t_ld] + stb_free[s], out=stb[s][:, 0:n], in_=st32[s][:, 0:n])
            st_free[s] = [t_c]
            t_st = POOL.dma(lambda e, s=s, n=n, idx=l * NU + u: e.dma_start(out=wscr[idx, :, 0:n], in_=stb[s][:, 0:n]),
                            st_sem[s], deps=[t_c])
            stb_free[s] = [t_st]

        mod_items = [(l, nb) for l in range(depth) for nb in range(24)]
        conv_items = [(l, u) for l in range(depth) for u in range(NU)]
        mi = ci = 0
        while mi < len(mod_items) or ci < len(conv_items):
            for _ in range(2):
                if ci < len(conv_items):
                    conv_unit(*conv_items[ci]); ci += 1
            if mi < len(mod_items):
                mod_block(*mod_items[mi]); mi += 1
        t_mod_last = pst8["t_mod_last"]
        prologue_done = [t_mod_last, t_idb, t_ones, t_o640, t_wsb, t_wpb, t_nm0, t_nm] + \
            [(st_sem[i].sem, st_sem[i].cnt) for i in range(3) if st_sem[i].cnt > 0] + \
            [st_free[i] for i in range(3)]

        t_c0 = op(DVE, "memset", prologue_done, C_b[0][:, 0:1], 0.0)
        t_c0 = op(DVE, "memset", [t_c0], C_b[1][:, 0:1], 0.0)
        prologue_done = prologue_done + [t_c0]
        if stage == 2:
            return finish(prologue_done)
        order = [(l, u) for _t in range(ntiles) for l in range(depth) for u in range(NU)]
        wsem = [S.dsem(f"w{i}") for i in range(NSLOT)]
        wst = {"issued": 0, "ld": {}, "done": {}}

        def w_issue_upto(k):
            k = min(k, len(order) - 1)
            while wst["issued"] <= k:
                i = wst["issued"]
                l, u = order[i]
                n = 1408 if u >= 35 else 2048
                s = i % NSLOT
                deps = list(prologue_done) if i < NSLOT else [wst["done"][i - NSLOT]]
                wst["ld"][i] = SP.dma(lambda e, s=s, n=n, idx=l * NU + u: e.dma_start(out=wring[:, s, 0:n], in_=wscr[idx, :, 0:n]),
                                      wsem[s], deps=deps)
                wst["issued"] += 1

        wpos = {"i": 0}

        def w_get():
            i = wpos["i"]
            wpos["i"] += 1
            w_issue_upto(i + NSLOT - 1)
            return wring[:, i % NSLOT, :], wst["ld"][i], i

        def w_done(i, tok):
            wst["done"][i] = tok

        tk_xT = [Tk() for _ in range(8)]
        tk_hT = Tk()
        hT_w = [None] * 8
        tk_mT = [Tk() for _ in range(8)]
        tk_U = Tk()
        tk_br = [Tk() for _ in range(8)]
        tk_qT = [Tk() for _ in range(4)]
        tk_Kc = [Tk() for _ in range(4)]
        tk_Kh = Tk()
        tk_Vc = [Tk() for _ in range(4)]
        tk_Vh = Tk()
        tk_pT = Tk()
        tk_ph = Tk()
        tk_gu = [Tk(), Tk()]
        tk_vn = [Tk() for _ in range(4)]
        tk_sq = [Tk() for _ in range(3)]
        tk_rstd = [Tk(), Tk()]
        tk_silu = [Tk(), Tk()]
        tk_act = [Tk() for _ in range(NF)]
        tk_att = dict(e=[Tk() for _ in range(3)], sp=[Tk() for _ in range(3)], C=[Tk(), Tk()], w=[Tk(), Tk()], wT=[Tk(), Tk(), Tk()])
        tk_pool = dict(L1=Tk(), L2=Tk(), pooled=Tk())
        tk_gv = [Tk() for _ in range(4)]
        tk_gvc = [Tk(), Tk()]
        kv_sem = S.dsem("kvs")
        kv_ld = S.dsem("kvl")
        x_ld = S.dsem("xld")
        o_st = S.dsem("ost")
        save_tok = {}
        sq_i = {"i": 0}
        rs_i = {"i": 0}

        def rms_stat(srcs, nfeat, src_deps):
            bank, btk = nsmall()
            n = len(srcs)
            tl = None
            for i, (ap, dep) in enumerate(zip(srcs, src_deps)):
                sq_i["i"] += 1
                k = sq_i["i"] % 3
                t_sq = A(sqb[:, k, :], ap, AF.Square, deps=[dep] + tk_sq[k].wr())
                tk_sq[k].wrote(t_sq)
                tl = MM(bank[:, :], lhsT=onesb[:], rhs=sqb[:, k, :], start=(i == 0), stop=(i == n - 1),
                        deps=[t_sq] + (btk.wr() if i == 0 else []), inc=True)
                tk_sq[k].read(tl)
            btk.wrote(tl)
            rs_i["i"] += 1
            r = rs_i["i"] % 2
            t1 = A(rstd[:, r, :], bank[:, :], AF.Ln, deps=[tl] + tk_rstd[r].wr(), scale=1.0 / nfeat, bias=EPS)
            btk.read(t1)
            t2 = A(rstd[:, r, :], rstd[:, r, :], AF.Exp, deps=[t1], scale=-0.5)
            tk_rstd[r].wrote(t2)
            return r, t2

        def norm_mod(l, Acol, Bcol, lastU=None):
            r, t_r = rms_stat([xT[:, c, :] for c in range(8)], D, [tk_xT[c].w for c in range(8)])
            tl = None
            for c in range(8):
                k = c % 2
                t1 = op(DVE, "tensor_tensor", [t_r, tk_xT[c].w] + tk_silu[k].wr(), out=silu[:, k, :], in0=xT[:, c, :],
                        in1=rstd[:, r, :], op=ALU.mult)
                tk_rstd[r].read(t1)
                tk_xT[c].read(t1)
                t2 = A(hT[:, c, :], silu[:, k, :], AF.Identity, deps=[t1] + (tk_hT.wr() if c == 0 else []),
                       scale=Acol(c), bias=Bcol(c))
                tk_silu[k].wrote(t1)
                tk_silu[k].read(t2)
                hT_w[c] = t2
                tl = t2
            tk_hT.wrote(tl)
            return tl

        class _Stop(Exception):
            pass

        def chk(k, toks):
            if stage == k:
                raise _Stop(toks)

        out_toks = []
        xstate = {}
        try:
            for tile in range(ntiles):
                tok0 = tile * TT
                att_tks = tk_att["e"] + tk_att["sp"] + [tk_att["C"][0]]
                if tile == 0:
                    xstate["t_x"] = POOL.dma(lambda e: e.dma_start(out=xin2, in_=x_d[0:TT, :].rearrange("(t p) n -> p t n", p=128)),
                                             x_ld, deps=list(prologue_done))
                t_x = xstate["t_x"]
                tl_x = None
                for c in range(8):
                    bank, btk = nsmall()
                    tl = None
                    for tb in range(4):
                        tl = TR(bank[:, tb * 128:(tb + 1) * 128], xin2[:, tb, c * 128:(c + 1) * 128], ident32[:],
                                deps=[t_x] + (btk.wr() if tb == 0 else []), inc=(tb == 3))
                    btk.wrote(tl)
                    t_e = evac(xT[:, c, :], bank[:, :], [tl] + tk_xT[c].wr())
                    btk.read(t_e)
                    tk_xT[c].wrote(t_e)
                    tl_x = tl
                t_z0 = op(DVE, "memset", [tl_x], C_b[0][:, 0:1], 0.0)
                for tk_ in att_tks:
                    tk_.wrote([tl_x, t_z0])
                chk(3, [tk_xT[c].w for c in range(8)])

                for l in range(depth):
                    first_tile = (tile == 0)
                    if not first_tile:
                        t_kh = POOL.dma(lambda e, l=l: e.dma_start(out=KTw[:, :, 0:512], in_=kscr[l]), kv_ld, deps=tk_Kh.wr() + save_tok[l])
                        t_vh = POOL.dma(lambda e, l=l: e.dma_start(out=Vw[:, 0:4, :], in_=vscr[l]), kv_ld, deps=tk_Vh.wr() + save_tok[l])
                        t_ph = POOL.dma(lambda e, l=l: e.dma_start(out=pT[:, :, 0:16], in_=pscr[l]), kv_ld, deps=tk_ph.wr() + save_tok[l])
                        t_halo = (kv_ld.sem, kv_ld.cnt)
                        tk_Kh.wrote(t_halo); tk_Vh.wrote(t_halo); tk_ph.wrote(t_halo)
                    else:
                        t_ph = op(POOL, "memset", tk_ph.wr(), pT[:, :, 0:16], 0.0)
                        tk_ph.wrote(t_ph)

                    norm_mod(l, lambda c: A1[:, l, c:c + 1], lambda c: modT[:, l, c:c + 1])

                    chk(4, [tk_hT.w])
                    gv_tok = {}
                    v_ev = {}
                    for u in range(9):
                        v_rd = []
                        ws, t_w, wi = w_get()
                        wv = ws.rearrange("p (kc n) -> p kc n", kc=8)
                        tl_u = None
                        if u in (0, 1, 2, 3, 6, 7):
                            for sub in range(2):
                                bank, btk = nsmall()
                                tl = None
                                for kc in range(8):
                                    tl = MM(bank[:, :], lhsT=wv[:, kc, sub * 128:(sub + 1) * 128], rhs=hT[:, kc, :],
                                            start=(kc == 0), stop=(kc == 7),
                                            deps=[t_w, hT_w[kc]] + (btk.wr() if kc == 0 else []), inc=(kc == 7))
                                btk.wrote(tl)
                                tl_u = tl
                                if u in (0, 1):
                                    c = u * 2 + sub
                                    t_e = A(qT[:, c, :], bank[:, :], AF.Identity, deps=[tl] + tk_qT[c].wr(), scale=0.125)
                                    tk_qT[c].wrote(t_e)
                                elif u in (2, 3):
                                    c = (u - 2) * 2 + sub
                                    t_e = evac(KTw[:, c, 512:1024], bank[:, :], [tl] + tk_Kc[c].wr())
                                    tk_Kc[c].wrote(t_e)
                                elif u == 6:
                                    t_e = op(DVE, "tensor_copy", [tl] + tk_pT.wr(), out=pT[:, sub, 16:528], in_=bank[:, :])
                                    if sub == 1:
                                        tk_pT.wrote(t_e)
                                else:
                                    t_e = A(guT[:, sub, :], bank[:, :], AF.Gelu_apprx_tanh, deps=[tl] + tk_gu[sub].wr())
                                    tk_gu[sub].wrote(t_e)
                                btk.read(t_e)
                        else:
                            bank, btk = nbig()
                            for tb in range(4):
                                tl = None
                                for kc in range(8):
                                    tl = MM(bank[:, tb * 256:(tb + 1) * 256], lhsT=hT[:, kc, tb * 128:(tb + 1) * 128], rhs=wv[:, kc, :],
                                            start=(kc == 0), stop=(kc == 7),
                                            deps=[t_w, hT_w[kc]] + (btk.wr() if (kc == 0 and tb == 0) else []), inc=(kc == 7))
                                tl_u = tl
                                if u == 4 and tb == 0:
                                    chk(60, [tl])
                                if tb % 2 == 0:
                                    continue
                                if u == 4 and tb == 1:
                                    chk(62, [tl])
                                for tb2 in (tb - 1, tb):
                                    if u in (4, 5):
                                        half = u - 4
                                        t_e = evac(Vw[:, 4 + tb2, half * 256:(half + 1) * 256], bank[:, tb2 * 256:(tb2 + 1) * 256],
                                                   [tl] + tk_Vc[tb2].wr(), eng=("act" if (tb // 2 + u) % 2 == 0 else "dve"))
                                        if half == 0:
                                            v_ev[tb2] = t_e
                                        else:
                                            tk_Vc[tb2].wrote([v_ev[tb2], t_e])
                                        v_rd.append(t_e)
                                    else:
                                        gv_tok[tb2] = (bank, btk, tl)
                                if u == 4 and tb == 1:
                                    chk(61, [tl] + v_rd)
                            btk.wrote(tl_u)
                            for t_ in v_rd:
                                btk.read(t_)
                        tk_hT.read(tl_u)
                        w_done(wi, tl_u)
                        chk(50 + u, [tl_u, tk_qT[0].w, tk_qT[1].w, tk_qT[2].w, tk_qT[3].w] + [tk_Kc[c_].w for c_ in range(4)] + [tk_Vc[c_].w for c_ in range(4)] + [tk_pT.w, tk_gu[0].w, tk_gu[1].w])

                    chk(5, [tk_qT[c].w for c in range(4)] + [tk_Kc[c].w for c in range(4)] + [tk_Vc[c].w for c in range(4)] + [tk_pT.w, tk_gu[0].w, tk_gu[1].w, tl_u])
                    if tile < ntiles - 1:
                        dk = [tk_Kc[c].w for c in range(4)] + [tk_Kh.w]
                        dv = [tk_Vc[tb].w for tb in range(4)] + [tk_Vh.w]
                        t1 = POOL.dma(lambda e, l=l: e.dma_start(out=kscr[l], in_=KTw[:, :, 512:1024]), kv_sem, deps=dk)
                        t2 = POOL.dma(lambda e, l=l: e.dma_start(out=vscr[l], in_=Vw[:, 4:8, :]), kv_sem, deps=dv)
                        t3 = POOL.dma(lambda e, l=l: e.dma_start(out=pscr[l], in_=pT[:, :, 512:528]), kv_sem, deps=[tk_pT.w, tk_ph.w])
                        t_sv = (kv_sem.sem, kv_sem.cnt)
                        save_tok[l] = [t_sv]
                        for c in range(4):
                            tk_Kc[c].read(t_sv)
                            tk_Vc[c].read(t_sv)
                        tk_pT.read(t_sv)

                    t_vs = {}
                    t_gs = {}
                    for tb in range(4):
                        bank, btk, tlv = gv_tok[tb]
                        t_g = A(gv[:, tb, :], bank[:, tb * 256:(tb + 1) * 256], AF.Gelu_apprx_tanh, deps=[tlv] + tk_gv[tb].wr(),
                                accum_out=lnst[:, tb, 0:1])
                        btk.read(t_g)
                        t_gs[tb] = t_g
                    sg_tmp = {}

                    def sg_p1(tb):
                        k = tb % 2
                        t_m = op(DVE, "tensor_scalar", [t_gs[tb]], out=lnst[:, tb, 1:2], in0=lnst[:, tb, 0:1], scalar1=-1.0 / 256, scalar2=None,
                                 op0=ALU.mult)
                        t_c = op(DVE, "tensor_scalar", [t_m] + tk_gvc[k].wr(), out=gvc[:, k, :], in0=gv[:, tb, :], scalar1=lnst[:, tb, 1:2], scalar2=None,
                                 op0=ALU.add)
                        tk_gvc[k].wrote(t_c)
                        sg_tmp[tb] = t_c

                    def sg_p2(tb):
                        k = tb % 2
                        t_s = A(gv[:, tb, :], gvc[:, k, :], AF.Square, deps=[sg_tmp[tb]], accum_out=lnst[:, tb, 2:3])
                        t_q0 = A(lnst[:, tb, 3:4], lnst[:, tb, 2:3], AF.Ln, deps=[t_s], scale=1.0 / 256, bias=EPS)
                        t_q = A(lnst[:, tb, 4:5], lnst[:, tb, 3:4], AF.Exp, deps=[t_q0], scale=-0.5)
                        tk_gvc[k].read(t_s)
                        sg_tmp[tb] = t_q

                    def sg_p3(tb):
                        k = tb % 2
                        t_v = op(DVE, "scalar_tensor_tensor", [sg_tmp[tb]] + tk_vn[tb].wr(), out=vn[:, tb, :], in0=gvc[:, k, :],
                                 scalar=lnst[:, tb, 4:5], in1=bc[:, l * 256:(l + 1) * 256], op0=ALU.mult, op1=ALU.mult)
                        tk_gvc[k].read(t_v)
                        tk_gv[tb].wrote(t_v)
                        tk_vn[tb].wrote(t_v)
                        t_vs[tb] = t_v

                    sg_sched = {}
                    for tb in range(4):
                        sg_sched[3 + 7 * tb] = (sg_p1, tb)
                        sg_sched[4 + 7 * tb] = (sg_p2, tb)
                        sg_sched[5 + 7 * tb] = (sg_p3, tb)

                    def sg_mix():
                        gbank = [nsmall(), nsmall()]
                        t_gl = None
                        for tb in range(4):
                            tl = None
                            for h in range(4):
                                gb, gtk = gbank[h // 2]
                                hp = h % 2
                                tl = MM(gb[hp * 64:(hp + 1) * 64, tb * 128:(tb + 1) * 128], lhsT=vn[:, tb, h * 64:(h + 1) * 64],
                                        rhs=wsTb[:, l * 4 + h, :], deps=[t_vs[tb]] + (gtk.wr() if (tb == 0 and hp == 0) else []), inc=(h == 3))
                            tk_vn[tb].read(tl)
                            t_gl = tl
                        for c in range(2):
                            gb, gtk = gbank[c]
                            gtk.wrote(t_gl)
                            bsb_ap = bass.AP(bc.tensor if hasattr(bc, "tensor") else bc, 1024 + (l * 2 + c) * 128, [[2048, 128], [0, 4], [1, 128]])
                            t1 = op(DVE, "tensor_tensor", [t_gl] + tk_br[6 + c].wr() + tk_U.wr(), out=brT[:, 6 + c, :].rearrange("p (t n) -> p t n", t=4),
                                    in0=gb[:, :].rearrange("p (t n) -> p t n", t=4), in1=bsb_ap, op=ALU.add)
                            gtk.read(t1)
                            t2 = op(DVE, "tensor_tensor", [t1, tk_gu[c].w], out=brT[:, 6 + c, :], in0=brT[:, 6 + c, :], in1=guT[:, c, :], op=ALU.mult)
                            tk_gu[c].read(t2)
                            tk_br[6 + c].wrote(t2)

                    dpp = [tk_pT.w, tk_ph.w]
                    t_l1 = op(DVE, "tensor_tensor", dpp + tk_pool["L1"].wr(), out=L1[:, :, 1:528], in0=pT[:, :, 1:528], in1=pT[:, :, 0:527], op=ALU.add)
                    t_l2a = op(DVE, "tensor_tensor", [t_l1] + tk_pool["L2"].wr(), out=L2[64:128, 0, 3:528], in0=L1[64:128, 0, 3:528], in1=L1[64:128, 0, 1:526], op=ALU.add)
                    t_l2b = op(DVE, "tensor_tensor", [t_l1, t_l2a], out=L2[:, 1, 3:528], in0=L1[:, 1, 3:528], in1=L1[:, 1, 1:526], op=ALU.add)
                    t_l3 = op(DVE, "tensor_tensor", [t_l2b], out=L1[:, 1, 7:528], in0=L2[:, 1, 7:528], in1=L2[:, 1, 3:524], op=ALU.add)
                    t_l4 = op(DVE, "tensor_tensor", [t_l3], out=L2[64:128, 1, 15:528], in0=L1[64:128, 1, 15:528], in1=L1[64:128, 1, 7:520], op=ALU.add)
                    fin = [(L1, 0, 0, 2), (L2, 1, 0, 4), (L1, 0, 1, 8), (L2, 1, 1, 16)]
                    tp = None
                    for (Lb, hp, c, w) in fin:
                        pr = slice(hp * 64, hp * 64 + 64)
                        tp = op(DVE, "scalar_tensor_tensor", [t_l4] + tk_pool["pooled"].wr(), out=pooledT[pr, c, :], in0=Lb[pr, c, 16:528],
                                scalar=1.0 / w, in1=pT[pr, c, 16:528], op0=ALU.mult, op1=ALU.subtract)
                        if first_tile:
                            t_a = op(DVE, "tensor_tensor", [t_l4, tp], out=Lb[pr, c, 16:32], in0=Lb[pr, c, 16:32], in1=pcorr[pr, c, :], op=ALU.mult)
                            tp = op(DVE, "tensor_tensor", [t_a, tp], out=pooledT[pr, c, 0:16], in0=Lb[pr, c, 16:32], in1=pT[pr, c, 16:32], op=ALU.subtract)
                    tk_pool["L1"].wrote(tp); tk_pool["L2"].wrote(tp); tk_pool["pooled"].wrote(tp)
                    tk_pT.read(tp); tk_ph.read(tp)

                    def pool_mix():
                        for c in range(2):
                            bank, btk = nsmall()
                            MM(bank[0:64, :], lhsT=wpb[0:64, l * 2 + c, :], rhs=pooledT[0:64, c, :], deps=[tp] + btk.wr())
                            tl = MM(bank[64:128, :], lhsT=wpb[64:128, l * 2 + c, :], rhs=pooledT[64:128, c, :], deps=[tp], inc=True)
                            btk.wrote(tl)
                            tk_pool["pooled"].read(tl)
                            t_e = A(brT[:, 4 + c, :], bank[:, :], AF.Identity, deps=[tl] + tk_br[4 + c].wr() + tk_U.wr(),
                                    scale=pp[:, PP_PS + l * 2 + c:PP_PS + l * 2 + c + 1])
                            btk.read(t_e)
                            tk_br[4 + c].wrote(t_e)

                    def branch_norm(c0, ncn, nfeat):
                        r, t_r = rms_stat([brT[:, c0 + i, :] for i in range(ncn)], nfeat, [tk_br[c0 + i].w for i in range(ncn)])
                        for i in range(ncn):
                            c = c0 + i
                            t_m = op(DVE, "scalar_tensor_tensor", [t_r, tk_br[c].w] + tk_mT[c].wr(), out=mT[:, c, :], in0=brT[:, c, :],
                                     scalar=pp[:, PP_MN + l * 8 + c:PP_MN + l * 8 + c + 1], in1=rstd[:, r, :], op0=ALU.mult, op1=ALU.mult)
                            tk_rstd[r].read(t_m)
                            tk_br[c].read(t_m)
                            tk_mT[c].wrote(t_m)
                            tk_U.read(t_m)

                    nhb = 32
                    st = [dict() for _ in range(nhb)]
                    abank = {}

                    def col0_of(qb):
                        return max(0, 128 * (4 - qb)) if first_tile else 0

                    def S0(g0):
                        hd = []
                        for g in (g0, g0 + 1):
                            qb, h = g // 8, g % 8
                            hp, c = h % 2, h // 2
                            zb, ztk = nbig()
                            hd.append(dict(g=g, qb=qb, h=h, hp=hp, c=c, pr=slice(hp * 64, hp * 64 + 64), zb=zb, ztk=ztk,
                                           q_ap=qT[slice(hp * 64, hp * 64 + 64), c, qb * 128:(qb + 1) * 128], first=True))
                        qb = hd[0]["qb"]
                        col0 = col0_of(qb)
                        if col0 == 0:
                            regions = [(0, 128, True), (128, 512, False), (512, 640, True)]
                        elif col0 < 512:
                            regions = [(col0, 512, False), (512, 640, True)]
                        else:
                            regions = [(512, 640, True)]
                        for (r0, r1, masked) in regions:
                            for H in hd:
                                c = H["c"]
                                d = ([tk_qT[c].w, tk_Kc[c].w, tk_Kh.w] + H["ztk"].wr()) if H["first"] else []
                                H["first"] = False
                                H["tl"] = MM(H["zb"][:, r0:r1], lhsT=H["q_ap"], rhs=KTw[H["pr"], c, qb * 128 + r0:qb * 128 + r1], start=True, stop=(not masked), deps=d)
                            if masked:
                                for H in hd:
                                    mcol = 0 if r0 == 0 else 128
                                    H["tl"] = MM(H["zb"][:, r0:r1], lhsT=identb[:], rhs=negmb[:, mcol:mcol + 128], start=False, stop=True, inc=(r0 == 512))
                        for H in hd:
                            tl, c = H["tl"], H["c"]
                            H["ztk"].wrote(tl)
                            tk_qT[c].read(tl); tk_Kc[c].read(tl); tk_Kh.read(tl)
                            st[H["g"]].update(zb=H["zb"], ztk=H["ztk"], tz=tl, col0=col0, qb=H["qb"], h=H["h"], hp=H["hp"], c=c)

                    def S1(g):
                        s = st[g]
                        k3 = g % 3
                        sl = slice(s["col0"], 640)
                        Te, Tsp = tk_att["e"][k3], tk_att["sp"][k3]
                        t_e = A(e_buf[k3][:, sl], s["zb"][:, sl], AF.Exp, deps=[s["tz"]] + Te.wr())
                        s["ztk"].read(t_e)
                        Te.wrote(t_e)
                        t_sp = A(sp_buf[k3][:, sl], e_buf[k3][:, sl], AF.Ln, deps=[t_e] + Tsp.wr(), bias=1.0)
                        Tsp.wrote(t_sp)

                    def S2(g):
                        s = st[g]
                        k, k3 = g % 2, g % 3
                        col0 = s["col0"]
                        sl = slice(col0, 640)
                        Tsp, TC = tk_att["sp"][k3], tk_att["C"][k]
                        dz = []
                        if col0 > 0:
                            dz = [op(DVE, "memset", TC.wr(), C_b[k][:, col0:col0 + 1], 0.0)]
                        t_c = op(DVE, "tensor_tensor_scan", [Tsp.w] + TC.wr() + dz, out=C_b[k][:, col0 + 1:641], data0=ones640[:, sl],
                                 data1=sp_buf[k3][:, sl], initial=0.0, op0=ALU.mult, op1=ALU.subtract)
                        TC.wrote(t_c)
                        Tsp.read(t_c)

                    def S2b(g):
                        s = st[g]
                        k, k3 = g % 2, g % 3
                        col0 = s["col0"]
                        sl = slice(col0, 640)
                        Te, Tsp, Tw, TC = tk_att["e"][k3], tk_att["sp"][k3], tk_att["w"][k], tk_att["C"][k]
                        t_t = A(sp_buf[k3][:, sl], C_b[k][:, col0:640], AF.Exp, deps=[TC.w] + Tsp.wr(), scale=-1.0, bias=C_b[k][:, 640:641])
                        TC.read(t_t)
                        Tsp.wrote(t_t)
                        t_w = op(DVE, "tensor_tensor", [Tsp.w, Te.w] + Tw.wr(), out=w_b[k][:, sl], in0=e_buf[k3][:, sl], in1=sp_buf[k3][:, sl], op=ALU.mult)
                        Te.read(t_w)
                        Tsp.read(t_w)
                        Tw.wrote(t_w)

                    def S3(g):
                        s = st[g]
                        k = g % 2
                        k3 = g % 3
                        Tw, TwT = tk_att["w"][k], tk_att["wT"][k3]
                        col0 = s["col0"]
                        tl = None
                        first = True
                        for kc in range(col0 // 128, 5):
                            tl = TR(psT[:, kc * 128:(kc + 1) * 128], w_b[k][:, kc * 128:(kc + 1) * 128], identb[:],
                                    deps=[Tw.w] + (psT_tk.wr() if first else []), inc=(kc == 4))
                            first = False
                        psT_tk.wrote(tl)
                        Tw.read(tl)
                        t_cp = evac(wT_b[k3][:, col0:640], psT[:, col0:640], [tl] + TwT.wr(), eng=("act" if g % 2 else "dve"))
                        psT_tk.read(t_cp)
                        TwT.wrote(t_cp)

                    def S4(g0):
                        s0 = st[g0]
                        qb, col0, c = s0["qb"], s0["col0"], s0["c"]
                        if s0["h"] == 0:
                            abank[qb] = aT_bank
                        ab, atk = abank[qb]
                        kcs = list(range(col0 // 128, 5))
                        tls = {}
                        for i, kc in enumerate(kcs):
                            blk = qb + kc
                            dv = [tk_Vh.w] if blk < 4 else [tk_Vc[blk - 4].w]
                            for g in (g0, g0 + 1):
                                h, hp = st[g]["h"], st[g]["hp"]
                                TwT = tk_att["wT"][g % 3]
                                tls[g] = MM(ab[hp * 64:(hp + 1) * 64, c * 128:(c + 1) * 128], lhsT=Vw[:, blk, h * 64:(h + 1) * 64],
                                            rhs=wT_b[g % 3][:, kc * 128:(kc + 1) * 128], start=(i == 0), stop=(i == len(kcs) - 1),
                                            deps=[TwT.w] + dv + (atk.wr() if (h == 0 and i == 0) else []), inc=(i == len(kcs) - 1))
                        for g in (g0, g0 + 1):
                            tk_att["wT"][g % 3].read(tls[g])
                        tl = tls[g0 + 1]
                        for blk in range(qb + kcs[0], qb + 5):
                            (tk_Vh if blk < 4 else tk_Vc[blk - 4]).read(tl)
                        if st[g0 + 1]["h"] == 7:
                            atk.wrote(tl)
                            t_e = evac(brT[:, 0:4, qb * 128:(qb + 1) * 128], ab[:, :].rearrange("p (c n) -> p c n", c=4),
                                       [tl] + tk_br[0].wr() + tk_U.wr())
                            atk.read(t_e)
                            st[g0 + 1]["t_out"] = t_e

                    for step in range(nhb + 6):
                        for si, fn in ((2, S2), (3, S2b), (1, S1), (4, S3)):
                            g = step - si
                            if 0 <= g < nhb:
                                fn(g)
                        g = step - 5
                        if 0 <= g < nhb and g % 2 == 1:
                            S4(g - 1)
                        if step < nhb and step % 2 == 0:
                            S0(step)
                        if step in sg_sched:
                            sg_sched[step][0](sg_sched[step][1])
                        if step == 28:
                            sg_mix()
                            pool_mix()
                        if step == 30:
                            branch_norm(4, 2, 256)
                        if step == 32:
                            branch_norm(6, 2, 256)
                    t_attn = [st[qb * 8 + 7]["t_out"] for qb in range(4)]
                    for c in range(4):
                        tk_br[c].wrote(t_attn)

                    chk(8, [tk_br[c].w for c in range(8)])
                    branch_norm(0, 4, 512)
                    if l == depth - 1 and tile < ntiles - 1:
                        dpf = []
                        for tk_ in tk_att["e"] + tk_att["sp"] + [tk_att["C"][0]]:
                            dpf += tk_.wr()
                        xstate["t_x"] = POOL.dma(lambda e, tok1=tok0 + TT: e.dma_start(out=xin2, in_=x_d[tok1:tok1 + TT, :].rearrange("(t p) n -> p t n", p=128)),
                                                 x_ld, deps=dpf)

                    chk(10, [tk_mT[c].w for c in range(8)])
                    for j in range(4):
                        ws, t_w, wi = w_get()
                        wv = ws.rearrange("p (kc n) -> p kc n", kc=8)
                        tl = None
                        for sub in range(2):
                            o = j * 2 + sub
                            bank, btk = nsmall()
                            for ki, kc in enumerate((4, 5, 6, 7, 0, 1, 2, 3)):
                                tl = MM(bank[:, :], lhsT=wv[:, kc, sub * 128:(sub + 1) * 128], rhs=mT[:, kc, :], start=(ki == 0), stop=(ki == 7),
                                        deps=[t_w, tk_mT[kc].w] + (btk.wr() if ki == 0 else []), inc=(ki == 7))
                            btk.wrote(tl)
                            t_x2 = op(DVE, "scalar_tensor_tensor", [tl] + tk_xT[o].wr(), out=xT[:, o, :], in0=bank[:, :],
                                      scalar=modT[:, l, 16 + o:17 + o], in1=xT[:, o, :], op0=ALU.mult, op1=ALU.add)
                            btk.read(t_x2)
                            tk_xT[o].wrote(t_x2)
                        for kc in range(8):
                            tk_mT[kc].read(tl)
                        w_done(wi, tl)

                    chk(11, [tk_xT[c].w for c in range(8)])
                    norm_mod(l, lambda c: A2[:, l, c:c + 1], lambda c: modT[:, l, 24 + c:25 + c])

                    for f in range(NF):
                        ws, t_w, wi = w_get()
                        wv = ws.rearrange("p (kc g n) -> p kc g n", kc=8, g=2)
                        bank, btk = nbig()
                        tl = None
                        for g_ in range(2):
                            for kc in range(8):
                                tl = MM(bank[:, g_ * 512:(g_ + 1) * 512], lhsT=wv[:, kc, g_, :], rhs=hT[:, kc, :], start=(kc == 0), stop=(kc == 7),
                                        deps=[t_w, hT_w[kc]] + (btk.wr() if (kc == 0 and g_ == 0) else []), inc=(kc == 7 and g_ == 1))
                        btk.wrote(tl)
                        w_done(wi, tl)
                        k = f % 2
                        t_s = A(silu[:, k, :], bank[:, 0:512], AF.Silu, deps=[tl] + tk_silu[k].wr())
                        t_a = op(DVE, "tensor_tensor", [t_s] + tk_act[f].wr() + (tk_U.wr() if f == 0 else []), out=actT[:, f, :], in0=silu[:, k, :],
                                 in1=bank[:, 512:1024], op=ALU.mult)
                        btk.read(t_a)
                        tk_silu[k].wrote(t_s)
                        tk_silu[k].read(t_a)
                        tk_act[f].wrote(t_a)
                        t_gu_last = tl
                    tk_hT.read(t_gu_last)

                    chk(12, [tk_act[f].w for f in range(NF)])
                    t_dn = None
                    for o in range(8):
                        bank, btk = nsmall()
                        tl = None
                        for half in range(2):
                            ws, t_w, wi = w_get()
                            wv = ws[:, 0:1408].rearrange("p (f n) -> p f n", f=11)
                            for fi in range(11):
                                f = half * 11 + fi
                                tl = MM(bank[:, :], lhsT=wv[:, fi, :], rhs=actT[:, f, :], start=(f == 0), stop=(f == NF - 1),
                                        deps=[t_w, tk_act[f].w] + (btk.wr() if f == 0 else []), inc=(fi == 10))
                            w_done(wi, tl)
                        btk.wrote(tl)
                        t_x2 = op(DVE, "scalar_tensor_tensor", [tl] + tk_xT[o].wr(), out=xT[:, o, :], in0=bank[:, :],
                                  scalar=modT[:, l, 40 + o:41 + o], in1=xT[:, o, :], op0=ALU.mult, op1=ALU.add)
                        btk.read(t_x2)
                        tk_xT[o].wrote(t_x2)
                        t_dn = tl
                    for f in range(NF):
                        tk_act[f].read(t_dn)
                    tk_U.wrote(t_dn)
                    for c in range(8):
                        tk_br[c].wrote(t_dn)

                chk(13, [tk_xT[c].w for c in range(8)])
                r, t_r = rms_stat([xT[:, c, :] for c in range(8)], D, [tk_xT[c].w for c in range(8)])
                for cg in range(2):
                    t_xo = None
                    for i in range(4):
                        c = cg * 4 + i
                        t_xo = op(DVE, "scalar_tensor_tensor", [t_r, tk_xT[c].w] + tk_U.wr(), out=xo[:, i, :], in0=xT[:, c, :],
                                  scalar=pp[:, PP_FN + c:PP_FN + c + 1], in1=rstd[:, r, :], op0=ALU.mult, op1=ALU.mult)
                        tk_xT[c].read(t_xo)
                    tk_rstd[r].read(t_xo)
                    t_ev = None
                    for tb in range(4):
                        bank, btk = nsmall()
                        tl = None
                        for i in range(4):
                            tl = TR(bank[:, i * 128:(i + 1) * 128], xo[:, i, tb * 128:(tb + 1) * 128], ident32[:],
                                    deps=[t_xo] + (btk.wr() if i == 0 else []), inc=(i == 3))
                        btk.wrote(tl)
                        t_ev = evac(xout[:, tb, :], bank[:, :], [tl] + tk_U.wr())
                        btk.read(t_ev)
                        t_ev_all = t_ev
                        st_tb = POOL.dma(lambda e, tok0=tok0, tb=tb, cg=cg: e.dma_start(
                            out=out_d[tok0 + tb * 128:tok0 + (tb + 1) * 128, cg * 512:(cg + 1) * 512], in_=xout[:, tb, :]), o_st, deps=[t_ev])
                        out_toks.append(st_tb)
                    tk_U.wrote((o_st.sem, o_st.cnt))
                    tk_U.read(tl)

        except _Stop as ex:
            return finish(list(ex.args[0]))
        POOL.final_wait([(o_st.sem, o_st.cnt)])
        SP.final_wait([(o_st.sem, o_st.cnt)])
        S.run()
    return nc


_CACHE = {}


def _host_consts():
    ident = np.eye(128, dtype=np.float32)
    a = np.arange(128)[:, None]
    b = np.arange(640)[None, :]
    maskA = ((b >= a) & (b < a + 512)).astype(np.float32)
    s = np.arange(128)[:, None]
    t = np.arange(128)[None, :]
    maskS = (s <= t).astype(np.float32)
    pc = np.zeros((128, 2, 16), np.float32)
    wins = [2, 4, 8, 16]
    tt = np.arange(16)
    for g in range(4):
        hp, c = g % 2, g // 2
        pc[hp * 64:(hp + 1) * 64, c, :] = 1.0 / np.minimum(tt + 1, wins[g])
    return np.concatenate([ident, maskA, maskS, pc.reshape(128, 32)], axis=1)


def _fm(v, nchunk):
    v = np.asarray(v, np.float32)
    lead = v.shape[:-1]
    r = v.reshape(*lead, nchunk, 128)
    r = np.moveaxis(r, -1, 0)
    return r.reshape(128, -1)


def prep_shared(inputs, depth=DEPTH):
    g = lambda k: np.asarray(inputs[k], np.float32)
    bada = _fm(g("b_ada"), 48)
    nmi = _fm(g("norm_mix_in"), 8)
    nfi = _fm(g("norm_ffn_in"), 8)
    fn = _fm(g("final_norm"), 8)
    mn = _fm(g("mix_norm"), 8)
    psc = _fm(g("pool_scale"), 2)
    sgn = np.broadcast_to(g("sg_norm").reshape(1, DEPTH * 256), (128, DEPTH * 256))
    bs = g("b_s")
    bsb = np.zeros((128, DEPTH, 2, 128), np.float32)
    for l in range(DEPTH):
        for c in range(2):
            bsb[0:64, l, c, :] = bs[l, 2 * c][None, :]
            bsb[64:128, l, c, :] = bs[l, 2 * c + 1][None, :]
    bc = np.concatenate([sgn, bsb.reshape(128, DEPTH * 256)], axis=1)
    ws = g("w_s")
    wsT = np.ascontiguousarray(np.transpose(ws, (3, 0, 1, 2))).reshape(128, DEPTH * 4 * 128)
    wpool = g("w_pool")
    wp = np.zeros((128, DEPTH, 2, 64), np.float32)
    for l in range(DEPTH):
        for gi in range(4):
            hp, i = gi % 2, gi // 2
            wp[hp * 64:(hp + 1) * 64, l, i, :] = wpool[l, gi]
    shared = dict(bc=np.ascontiguousarray(bc), cst=_host_consts(), wsT=wsT, wp=wp.reshape(128, 512),
                  w_ada=g("w_ada"), w_in=g("w_in"), w_out=g("w_out"), w_gate_up=g("w_gate_up"), w_down=g("w_down"))
    pp_common = np.concatenate([bada, nmi, nfi, fn, mn, psc], axis=1)
    return shared, pp_common


def kernel(x, c, w_ada, b_ada, norm_mix_in, w_in, w_pool, pool_scale, sg_norm, w_s, b_s,
           mix_norm, w_out, norm_ffn_in, w_gate_up, w_down, final_norm):
    inputs = dict(x=x, c=c, w_ada=w_ada, b_ada=b_ada, norm_mix_in=norm_mix_in, w_in=w_in, w_pool=w_pool,
                  pool_scale=pool_scale, sg_norm=sg_norm, w_s=w_s, b_s=b_s, mix_norm=mix_norm, w_out=w_out,
                  norm_ffn_in=norm_ffn_in, w_gate_up=w_gate_up, w_down=w_down, final_norm=final_norm)
    x = np.asarray(x, np.float32)
    c = np.asarray(c, np.float32)
    B = x.shape[0]
    if "nc" not in _CACHE:
        _CACHE["nc"] = build_program()
    nc = _CACHE["nc"]
    shared, pp_common = prep_shared(inputs)
    in_maps = []
    for b in range(B):
        m = dict(shared)
        m["x"] = np.ascontiguousarray(x[b])
        m["pp"] = np.ascontiguousarray(np.concatenate([_fm(c[b], 8), pp_common], axis=1))
        in_maps.append(m)
    res = run_bass_kernel_spmd(nc, in_maps, core_ids=list(range(B)))
    return np.stack([np.asarray(r["out"], np.float32) for r in res.results], axis=0)
```
